# Optimizing a Trainium2 kernel written in Bass

```python
import math
import jax, jax.numpy as jnp
from jax import lax
import numpy as np

D_MODEL = 1024
BATCH = 8
SEQ = 2048
DEPTH = 2
DEC_BATCH = 128
DEC_SEQ = 8
PAST_LEN = 16384
PAGE_SIZE = 128

CONV_W = 4
N_BRANCH = 3
EPS = 1e-6
RG_WIDTH = D_MODEL // 2
RG_BLOCKS = 8
RG_BLOCK = RG_WIDTH // RG_BLOCKS
RG_C = 8.0
S5_WIDTH = D_MODEL // 2
S5_GROUP = 16
S5_GROUPS = S5_WIDTH // S5_GROUP
S5_STATE = 64
GDN_HEADS = 4
GDN_DK = 128
GDN_DV = 128
GDN_QK = GDN_HEADS * GDN_DK
GDN_V = GDN_HEADS * GDN_DV
QKV_WIDTH = 2 * GDN_QK + GDN_V
GDN_CHUNK = 64
D_FF = ((8 * D_MODEL + 3 * 256 - 1) // (3 * 256)) * 256
N_IN = 2 * RG_WIDTH + S5_WIDTH + QKV_WIDTH + GDN_V + 2 * GDN_HEADS + N_BRANCH * D_MODEL

kernel_name = "hybrid_rglru_s5_gdn_decode_step"


def split_points():
    sizes = (RG_WIDTH, RG_WIDTH, S5_WIDTH, QKV_WIDTH, GDN_V, GDN_HEADS, GDN_HEADS, N_BRANCH * D_MODEL)
    return [int(v) for v in np.cumsum(sizes)[:-1]]


def rmsnorm(x, g):
    xf = x.astype(jnp.float32)
    y = xf * lax.rsqrt(jnp.mean(xf * xf, axis=-1, keepdims=True) + EPS) * g.astype(jnp.float32)
    return y.astype(x.dtype)


def l2norm(x):
    return x * lax.rsqrt(jnp.sum(x * x, axis=-1, keepdims=True) + EPS)


def causal_conv(x, buf, w):
    T = x.shape[1]
    xp = jnp.concatenate([buf.astype(x.dtype), x], axis=1)
    y = xp[:, 0:T] * w[0]
    for k in range(1, CONV_W):
        y = y + xp[:, k:k + T] * w[k]
    return y, xp[:, -(CONV_W - 1):]


def linear_scan(a, b, h0):
    b = b.at[:, 0].add(a[:, 0] * h0)
    def combine(l, r):
        return l[0] * r[0], r[0] * l[1] + r[1]
    _, h = lax.associative_scan(combine, (a, b), axis=1)
    return h


def rglru_branch(xr, gate_in, conv_buf, h0, conv_w, conv_b, wa, ba, wx, bx, lam, w_o):
    B, T, _ = xr.shape
    xc, new_buf = causal_conv(xr, conv_buf, conv_w)
    xc = xc + conv_b
    xb = xc.reshape(B, T, RG_BLOCKS, RG_BLOCK)
    r = jax.nn.sigmoid(jnp.einsum("btni,nij->btnj", xb, wa).reshape(B, T, RG_WIDTH) + ba).astype(jnp.float32)
    i = jax.nn.sigmoid(jnp.einsum("btni,nij->btnj", xb, wx).reshape(B, T, RG_WIDTH) + bx).astype(jnp.float32)
    log_a = -RG_C * r * jax.nn.softplus(-lam.astype(jnp.float32))
    a = jnp.exp(log_a)
    inp = jnp.sqrt(-jnp.expm1(2.0 * log_a)) * (i * xc.astype(jnp.float32))
    h = linear_scan(a, inp, h0.astype(jnp.float32))
    y = h.astype(xr.dtype) * jax.nn.gelu(gate_in)
    return y @ w_o, new_buf, h[:, -1]


def s5_branch(u, s_re, s_im, a_re, a_im, b_re, b_im, c_re, c_im, d, log_dt, glu_w, glu_b, w_o):
    B, T, _ = u.shape
    f32 = jnp.float32
    lam = lax.complex(a_re.astype(f32), a_im.astype(f32))
    dt = jnp.exp(log_dt.astype(f32))[:, None]
    lam_bar = jnp.exp(lam * dt)
    b_bar = ((lam_bar - 1.0) / lam)[..., None] * lax.complex(b_re.astype(f32), b_im.astype(f32))
    c_c = lax.complex(c_re.astype(f32), c_im.astype(f32))
    ug = u.astype(f32).reshape(B, T, S5_GROUPS, S5_GROUP)
    bu = jnp.einsum("gph,btgh->btgp", b_bar, ug)
    a = jnp.broadcast_to(lam_bar, bu.shape)
    s = linear_scan(a, bu, lax.complex(s_re.astype(f32), s_im.astype(f32)))
    y = jnp.einsum("ghp,btgp->btgh", c_c, s).real + d.astype(f32).reshape(S5_GROUPS, S5_GROUP) * ug
    y = jax.nn.gelu(y.reshape(B, T, S5_WIDTH)).astype(u.dtype)
    y = y * jax.nn.sigmoid(y @ glu_w + glu_b)
    s_last = s[:, -1]
    return y @ w_o, s_last.real, s_last.imag


def chunk_gated_delta(q, k, v, beta, g, S0):
    B, T, H, _ = q.shape
    C = min(GDN_CHUNK, T)
    pad = (-T) % C
    if pad:
        def padt(t):
            return jnp.pad(t, [(0, 0), (0, pad)] + [(0, 0)] * (t.ndim - 2))
        q, k, v, beta, g = padt(q), padt(k), padt(v), padt(beta), padt(g)
    N = (T + pad) // C
    def chunks(t):
        t = t.reshape((B, N, C, H) + t.shape[3:])
        return jnp.swapaxes(jnp.moveaxis(t, 1, 0), 2, 3)
    q, k, v, beta, g = chunks(q), chunks(k), chunks(v), chunks(beta), chunks(g)
    gc = jnp.cumsum(g, axis=-1)
    kb = k * beta[..., None]
    vb = v * beta[..., None]
    causal = jnp.tril(jnp.ones((C, C), bool))
    strict = jnp.tril(jnp.ones((C, C), bool), -1)
    diff = gc[..., :, None] - gc[..., None, :]
    decay = jnp.where(causal, jnp.exp(jnp.where(causal, diff, 0.0)), 0.0)
    lmat = jnp.where(strict, jnp.einsum("nbhid,nbhjd->nbhij", kb, k) * decay, 0.0)
    eye = jnp.eye(C, dtype=jnp.float32)
    tmat = lax.linalg.triangular_solve(eye + lmat, jnp.broadcast_to(eye, lmat.shape), left_side=True, lower=True)
    u = tmat @ vb
    w = tmat @ (kb * jnp.exp(gc)[..., None])
    qk = jnp.where(causal, jnp.einsum("nbhid,nbhjd->nbhij", q, k) * decay, 0.0)

    def step(S, xs):
        q_i, k_i, u_i, w_i, qk_i, g_i = xs
        v_new = u_i - w_i @ S
        o_i = (q_i * jnp.exp(g_i)[..., None]) @ S + qk_i @ v_new
        g_last = g_i[..., -1]
        k_dec = k_i * jnp.exp(g_last[..., None] - g_i)[..., None]
        S = S * jnp.exp(g_last)[..., None, None] + jnp.einsum("bhcd,bhce->bhde", k_dec, v_new)
        return S, o_i

    S_last, o = lax.scan(step, S0, (q, k, u, w, qk, gc))
    o = jnp.transpose(o, (1, 0, 3, 2, 4)).reshape(B, N * C, H, GDN_DV)[:, :T]
    return o, S_last


def gdn_branch(qkv, z, a_in, b_in, conv_buf, S0, conv_w, a_log, dt_bias, norm_w, w_o):
    B, T, _ = qkv.shape
    f32 = jnp.float32
    qkv_c, new_buf = causal_conv(qkv, conv_buf, conv_w)
    qkv_c = jax.nn.silu(qkv_c).astype(f32)
    q, k, v = jnp.split(qkv_c, [GDN_QK, 2 * GDN_QK], axis=-1)
    q = l2norm(q.reshape(B, T, GDN_HEADS, GDN_DK)) * (GDN_DK ** -0.5)
    k = l2norm(k.reshape(B, T, GDN_HEADS, GDN_DK))
    v = v.reshape(B, T, GDN_HEADS, GDN_DV)
    beta = jax.nn.sigmoid(b_in.astype(f32))
    g = -jnp.exp(a_log.astype(f32)) * jax.nn.softplus(a_in.astype(f32) + dt_bias.astype(f32))
    o, S_last = chunk_gated_delta(q, k, v, beta, g, S0.astype(f32))
    o = rmsnorm(o, norm_w) * jax.nn.silu(z.astype(f32).reshape(B, T, GDN_HEADS, GDN_DV))
    return o.reshape(B, T, GDN_V).astype(qkv.dtype) @ w_o, new_buf, S_last


def decoder_layer(x, layer_state, l, p):
    conv_rg, h_rg, s5_re, s5_im, conv_gd, s_gd = layer_state
    h = rmsnorm(x, p["norm_pre_mix"][l])
    proj = h @ p["w_in"][l]
    rg_x, rg_gate, s5_u, qkv, z, a_in, b_in, gate_logits = jnp.split(proj, split_points(), axis=-1)
    y_rg, conv_rg, h_rg = rglru_branch(rg_x, rg_gate, conv_rg, h_rg, p["rg_conv_w"][l], p["rg_conv_b"][l],
                                       p["rg_wa"][l], p["rg_ba"][l], p["rg_wx"][l], p["rg_bx"][l],
                                       p["rg_lambda"][l], p["rg_w_out"][l])
    y_s5, s5_re, s5_im = s5_branch(s5_u, s5_re, s5_im, p["s5_a_re"][l], p["s5_a_im"][l], p["s5_b_re"][l],
                                   p["s5_b_im"][l], p["s5_c_re"][l], p["s5_c_im"][l], p["s5_d"][l],
                                   p["s5_log_dt"][l], p["s5_glu_w"][l], p["s5_glu_b"][l], p["s5_w_out"][l])
    y_gd, conv_gd, s_gd = gdn_branch(qkv, z, a_in, b_in, conv_gd, s_gd, p["gdn_conv_w"][l], p["gdn_a_log"][l],
                                     p["gdn_dt_bias"][l], p["gdn_norm_w"][l], p["gdn_w_out"][l])
    g_rg, g_s5, g_gd = jnp.split(jax.nn.sigmoid(gate_logits), N_BRANCH, axis=-1)
    mixed = ((g_rg * y_rg + g_s5 * y_s5 + g_gd * y_gd).astype(x.dtype)) @ p["w_out"][l]
    x = x + rmsnorm(mixed, p["norm_post_mix"][l])
    hf = rmsnorm(x, p["norm_pre_ffn"][l])
    f = (jax.nn.silu(hf @ p["ffn_w_gate"][l]) * (hf @ p["ffn_w_up"][l])) @ p["ffn_w_down"][l]
    x = x + rmsnorm(f, p["norm_post_ffn"][l])
    return x, (conv_rg, h_rg, s5_re, s5_im, conv_gd, s_gd)


def run_stack(x, states, p):
    new = [[] for _ in states]
    for l in range(DEPTH):
        x, layer_state = decoder_layer(x, tuple(s[l] for s in states), l, p)
        for lst, s in zip(new, layer_state):
            lst.append(s)
    return x, [jnp.stack(lst) for lst in new]


def zero_states(b, dtype):
    f32 = jnp.float32
    return (jnp.zeros((DEPTH, b, CONV_W - 1, RG_WIDTH), dtype),
            jnp.zeros((DEPTH, b, RG_WIDTH), f32),
            jnp.zeros((DEPTH, b, S5_GROUPS, S5_STATE), f32),
            jnp.zeros((DEPTH, b, S5_GROUPS, S5_STATE), f32),
            jnp.zeros((DEPTH, b, CONV_W - 1, QKV_WIDTH), dtype),
            jnp.zeros((DEPTH, b, GDN_HEADS, GDN_DK, GDN_DV), f32))


def setup_inputs(seed: int = 0) -> dict:
    key = jax.random.key(seed)
    ks = iter(jax.random.split(key, 64))
    f32 = jnp.float32
    L, D = DEPTH, D_MODEL

    def nrm(shape, scale):
        return jax.random.normal(next(ks), shape, f32) * scale

    def unif(shape, lo, hi):
        return jax.random.uniform(next(ks), shape, f32, lo, hi)

    x_prompt = nrm((BATCH, SEQ, D), 1.0)
    x_sample = nrm((DEC_BATCH, DEC_SEQ, D), 1.0)
    state_rg_conv = nrm((L, DEC_BATCH, CONV_W - 1, RG_WIDTH), 1.0)
    state_rg_h = nrm((L, DEC_BATCH, RG_WIDTH), 0.5)
    state_s5_re = nrm((L, DEC_BATCH, S5_GROUPS, S5_STATE), 0.1)
    state_s5_im = nrm((L, DEC_BATCH, S5_GROUPS, S5_STATE), 0.1)
    state_gdn_conv = nrm((L, DEC_BATCH, CONV_W - 1, QKV_WIDTH), 1.0)
    state_gdn_s = nrm((L, DEC_BATCH, GDN_HEADS, GDN_DK, GDN_DV), 0.5)

    norm_pre_mix = 1.0 + nrm((L, D), 0.01)
    norm_post_mix = 1.0 + nrm((L, D), 0.01)
    norm_pre_ffn = 1.0 + nrm((L, D), 0.01)
    norm_post_ffn = 1.0 + nrm((L, D), 0.01)
    w_in = nrm((L, D, N_IN), D ** -0.5)

    rg_conv_w = nrm((L, CONV_W, RG_WIDTH), CONV_W ** -0.5)
    rg_conv_b = nrm((L, RG_WIDTH), 0.01)
    rg_wa = nrm((L, RG_BLOCKS, RG_BLOCK, RG_BLOCK), RG_BLOCK ** -0.5)
    rg_ba = nrm((L, RG_WIDTH), 0.01)
    rg_wx = nrm((L, RG_BLOCKS, RG_BLOCK, RG_BLOCK), RG_BLOCK ** -0.5)
    rg_bx = nrm((L, RG_WIDTH), 0.01)
    a_pow = unif((L, RG_WIDTH), 0.9, 0.999)
    base = a_pow ** (1.0 / RG_C)
    rg_lambda = jnp.log(base) - jnp.log1p(-base)
    rg_w_out = nrm((L, RG_WIDTH, D), RG_WIDTH ** -0.5)

    n = jnp.arange(S5_STATE, dtype=f32)
    s5_a_re = -0.5 + nrm((L, S5_GROUPS, S5_STATE), 0.01)
    s5_a_im = jnp.pi * n + nrm((L, S5_GROUPS, S5_STATE), 0.01)
    s5_b_re = nrm((L, S5_GROUPS, S5_STATE, S5_GROUP), (2 * S5_GROUP) ** -0.5)
    s5_b_im = nrm((L, S5_GROUPS, S5_STATE, S5_GROUP), (2 * S5_GROUP) ** -0.5)
    s5_c_re = nrm((L, S5_GROUPS, S5_GROUP, S5_STATE), S5_STATE ** -0.5)
    s5_c_im = nrm((L, S5_GROUPS, S5_GROUP, S5_STATE), S5_STATE ** -0.5)
    s5_d = nrm((L, S5_WIDTH), 1.0)
    s5_log_dt = unif((L, S5_GROUPS), math.log(1e-3), math.log(1e-1))
    s5_glu_w = nrm((L, S5_WIDTH, S5_WIDTH), S5_WIDTH ** -0.5)
    s5_glu_b = nrm((L, S5_WIDTH), 0.01)
    s5_w_out = nrm((L, S5_WIDTH, D), S5_WIDTH ** -0.5)

    gdn_conv_w = nrm((L, CONV_W, QKV_WIDTH), CONV_W ** -0.5)
    gdn_a_log = jnp.log(unif((L, GDN_HEADS), 1.0, 16.0))
    dt0 = jnp.exp(unif((L, GDN_HEADS), math.log(1e-3), math.log(1e-1)))
    gdn_dt_bias = dt0 + jnp.log(-jnp.expm1(-dt0))
    gdn_norm_w = 1.0 + nrm((L, GDN_DV), 0.01)
    gdn_w_out = nrm((L, GDN_V, D), GDN_V ** -0.5)

    w_out = nrm((L, D, D), D ** -0.5)
    ffn_w_gate = nrm((L, D, D_FF), D ** -0.5)
    ffn_w_up = nrm((L, D, D_FF), D ** -0.5)
    ffn_w_down = nrm((L, D_FF, D), D_FF ** -0.5)

    return {
        "x_prompt": x_prompt, "x_sample": x_sample,
        "state_rg_conv": state_rg_conv, "state_rg_h": state_rg_h,
        "state_s5_re": state_s5_re, "state_s5_im": state_s5_im,
        "state_gdn_conv": state_gdn_conv, "state_gdn_s": state_gdn_s,
        "norm_pre_mix": norm_pre_mix, "norm_post_mix": norm_post_mix,
        "norm_pre_ffn": norm_pre_ffn, "norm_post_ffn": norm_post_ffn, "w_in": w_in,
        "rg_conv_w": rg_conv_w, "rg_conv_b": rg_conv_b, "rg_wa": rg_wa, "rg_ba": rg_ba,
        "rg_wx": rg_wx, "rg_bx": rg_bx, "rg_lambda": rg_lambda, "rg_w_out": rg_w_out,
        "s5_a_re": s5_a_re, "s5_a_im": s5_a_im, "s5_b_re": s5_b_re, "s5_b_im": s5_b_im,
        "s5_c_re": s5_c_re, "s5_c_im": s5_c_im, "s5_d": s5_d, "s5_log_dt": s5_log_dt,
        "s5_glu_w": s5_glu_w, "s5_glu_b": s5_glu_b, "s5_w_out": s5_w_out,
        "gdn_conv_w": gdn_conv_w, "gdn_a_log": gdn_a_log, "gdn_dt_bias": gdn_dt_bias,
        "gdn_norm_w": gdn_norm_w, "gdn_w_out": gdn_w_out,
        "w_out": w_out, "ffn_w_gate": ffn_w_gate, "ffn_w_up": ffn_w_up, "ffn_w_down": ffn_w_down,
    }


def reference(x_prompt, x_sample, state_rg_conv, state_rg_h, state_s5_re, state_s5_im, state_gdn_conv, state_gdn_s,
              norm_pre_mix, norm_post_mix, norm_pre_ffn, norm_post_ffn, w_in,
              rg_conv_w, rg_conv_b, rg_wa, rg_ba, rg_wx, rg_bx, rg_lambda, rg_w_out,
              s5_a_re, s5_a_im, s5_b_re, s5_b_im, s5_c_re, s5_c_im, s5_d, s5_log_dt, s5_glu_w, s5_glu_b, s5_w_out,
              gdn_conv_w, gdn_a_log, gdn_dt_bias, gdn_norm_w, gdn_w_out,
              w_out, ffn_w_gate, ffn_w_up, ffn_w_down):
    p = {
        "norm_pre_mix": norm_pre_mix, "norm_post_mix": norm_post_mix,
        "norm_pre_ffn": norm_pre_ffn, "norm_post_ffn": norm_post_ffn, "w_in": w_in,
        "rg_conv_w": rg_conv_w, "rg_conv_b": rg_conv_b, "rg_wa": rg_wa, "rg_ba": rg_ba,
        "rg_wx": rg_wx, "rg_bx": rg_bx, "rg_lambda": rg_lambda, "rg_w_out": rg_w_out,
        "s5_a_re": s5_a_re, "s5_a_im": s5_a_im, "s5_b_re": s5_b_re, "s5_b_im": s5_b_im,
        "s5_c_re": s5_c_re, "s5_c_im": s5_c_im, "s5_d": s5_d, "s5_log_dt": s5_log_dt,
        "s5_glu_w": s5_glu_w, "s5_glu_b": s5_glu_b, "s5_w_out": s5_w_out,
        "gdn_conv_w": gdn_conv_w, "gdn_a_log": gdn_a_log, "gdn_dt_bias": gdn_dt_bias,
        "gdn_norm_w": gdn_norm_w, "gdn_w_out": gdn_w_out,
        "w_out": w_out, "ffn_w_gate": ffn_w_gate, "ffn_w_up": ffn_w_up, "ffn_w_down": ffn_w_down,
    }
    y_prompt, (p_rg_conv, p_rg_h, p_s5_re, p_s5_im, p_gdn_conv, p_gdn_s) = run_stack(
        x_prompt, zero_states(x_prompt.shape[0], x_prompt.dtype), p)
    y_sample, (s_rg_conv, s_rg_h, s_s5_re, s_s5_im, s_gdn_conv, s_gdn_s) = run_stack(
        x_sample, (state_rg_conv, state_rg_h, state_s5_re, state_s5_im, state_gdn_conv, state_gdn_s), p)
    return (y_prompt, y_sample,
            p_rg_conv, p_rg_h, p_s5_re, p_s5_im, p_gdn_conv, p_gdn_s,
            s_rg_conv, s_rg_h, s_s5_re, s_s5_im, s_gdn_conv, s_gdn_s)
```

```python
import os
import numpy as np
from contextlib import ExitStack
import concourse.bass as bass
import concourse.mybir as mybir
from concourse.bass_utils import run_bass_kernel_spmd

F32 = mybir.dt.float32
BF16 = mybir.dt.bfloat16
F32R = mybir.dt.float32r
AF = mybir.ActivationFunctionType
ALU = mybir.AluOpType

NL = 2
D = 1024
KD = 8
NIN = 6664
DFF = 2816
KF = 22
NP_TOK = 2048
NS_TOK = 128
NT_TOK = NP_TOK + NS_TOK
NB = 16
EPS = 1e-6
GROUPS = [(0, 512, 1, 512), (512, 512, 1, 512), (1024, 512, 1, 512), (1536, 512, 1, 512),
          (2048, 128, 16, 8)]
C_RGX, C_RGG, C_S5U, C_QKV, C_Z, C_AB, C_GATE = 0, 512, 1024, 1536, 3072, 3584, 3592
WD_P = 32

PK = {}
_o = 0
for _n, _w in [("g_pre", 8), ("g_post", 8), ("g_pref", 8), ("g_postf", 8), ("rg_cw", 16), ("rg_cb", 4),
               ("rg_ba", 4), ("rg_bx", 4), ("rg_lam", 4), ("s5_d", 4), ("s5_glub", 4), ("gd_cw", 48),
               ("gd_nw", 1), ("s5_are", 16), ("s5_aim", 16), ("s5_ldt", 16), ("gd_alog", 4), ("gd_dtb", 4),
               ("w_ab", 64)]:
    PK[_n] = (_o, _w)
    _o += _w
NPK = _o


class Tile:
    __slots__ = ("h", "w", "r", "name", "psum")

    def __init__(self, h, name=""):
        self.h = h
        self.w = None
        self.r = {}
        self.name = name
        self.psum = False

    def __getitem__(self, idx):
        return Vw(self, self.h[idx])


class Vw:
    __slots__ = ("t", "a")

    def __init__(self, t, a):
        self.t = t
        self.a = a

    def __getitem__(self, idx):
        return Vw(self.t, self.a[idx])

    def re(self, pat, **kw):
        return Vw(self.t, self.a.rearrange(pat, **kw))

    def bc(self, pos, n):
        ap = [list(x) for x in self.a.ap]
        ap.insert(1 + pos, [0, n])
        return Vw(self.t, bass.AP(self.a.tensor, self.a.offset, ap))


def dram(ap):
    return Vw(None, ap)


class Sched:
    ENGS = ("tensor", "vector", "scalar", "gpsimd", "sync")

    def __init__(self, nc, es, n_dma_slots=8):
        self.nc = nc
        self.es = es
        self.prog = {e: [] for e in self.ENGS}
        self.sems = {}
        self.count = {}
        self.seen = {e: {} for e in self.ENGS}
        for e in self.ENGS:
            self.sems[e] = es.enter_context(nc.semaphore("s_" + e))
            self.count[e] = 0
        self.dma_slots = {}
        self.dma_next = {}
        for q in ("sync", "scalar", "gpsimd"):
            sl = []
            for i in range(n_dma_slots):
                key = "d_%s_%d" % (q, i)
                self.sems[key] = es.enter_context(nc.semaphore(key))
                self.count[key] = 0
                sl.append(key)
            self.dma_slots[q] = sl
            self.dma_next[q] = 0
        self.n_tiles = 0
        self.pools = {}
        self.out_deps = []
        self.sb_bytes = 0

    def sb(self, shape, dtype=F32, name=None):
        self.n_tiles += 1
        name = (name or "t") + "_%d" % self.n_tiles
        h = self.es.enter_context(self.nc.sbuf_tensor(name, list(shape), dtype))
        n = 1
        for s in shape[1:]:
            n *= s
        self.sb_bytes += n * (2 if dtype == BF16 else 4)
        return Tile(h, name)

    def ps(self, shape, dtype=F32, name=None):
        self.n_tiles += 1
        name = (name or "p") + "_%d" % self.n_tiles
        h = self.es.enter_context(self.nc.psum_tensor(name, list(shape), dtype))
        t = Tile(h, name)
        t.psum = True
        return t

    def alloc(self, kind):
        shape, dtype = {"F": ([128, 516], F32), "H": ([128, 512], BF16), "R": ([128, 2048], F32),
                        "C": ([128, 64], F32), "Q": ([128, 512], F32R)}[kind]
        fl = self.pools.setdefault(kind, [])
        if fl:
            return fl.pop(0)
        t = self.sb(shape, dtype, name=kind)
        t.name = kind
        self.pool_n = getattr(self, "pool_n", {})
        self.pool_n[kind] = self.pool_n.get(kind, 0) + 1
        return t

    def free(self, *tiles):
        for t in tiles:
            self.pools[t.name].append(t)

    def _deps(self, eng, reads, writes):
        deps = {}

        def add(k, v, skip_same):
            if k == eng and skip_same:
                return
            if deps.get(k, 0) < v:
                deps[k] = v

        for t in reads:
            if t is not None and t.w is not None:
                add(t.w[0], t.w[1], False)
        pe = (eng == "tensor")
        for t in writes:
            if t is None:
                continue
            if t.w is not None:
                add(t.w[0], t.w[1], pe)
            for k, v in t.r.items():
                add(k, v, pe)
        out = []
        seen = self.seen[eng]
        for k, v in deps.items():
            if seen.get(k, 0) < v:
                seen[k] = v
                out.append((k, v))
        return out

    def _mark(self, me, reads, writes):
        for t in writes:
            if t is not None:
                t.w = me
                t.r = {}
        for t in reads:
            if t is not None:
                if t.r.get(me[0], 0) < me[1]:
                    t.r[me[0]] = me[1]

    def op(self, eng, fn, reads=(), writes=()):
        pr = [t for t in reads if t is not None and t.psum]
        if pr:
            reads = [t for t in reads if t is None or not t.psum]
            writes = list(writes) + pr
        waits = self._deps(eng, reads, writes)
        self.count[eng] += 1
        me = (eng, self.count[eng])
        self.prog[eng].append((waits, fn, (eng, 1)))
        self._mark(me, reads, writes)

    def dma(self, q, fn, reads=(), writes=()):
        slots = self.dma_slots[q]
        key = slots[self.dma_next[q] % len(slots)]
        self.dma_next[q] += 1
        waits = self._deps(q, reads, writes)
        prev = self.count[key]
        if prev > 0 and self.seen[q].get(key, 0) < prev:
            self.seen[q][key] = prev
            waits.append((key, prev))
        self.count[key] += 16
        me = (key, self.count[key])
        self.prog[q].append((waits, fn, (key, 16)))
        self._mark(me, reads, writes)
        return me

    def wait_all(self, eng, deps):
        waits = []
        for k, v in deps:
            if self.seen[eng].get(k, 0) < v:
                self.seen[eng][k] = v
                waits.append((k, v))
        if waits:
            self.prog[eng].append((waits, None, None))

    def emit(self):
        nc = self.nc
        sems = self.sems
        with nc.Block() as block:
            for e in self.ENGS:
                prog = self.prog[e]
                if not prog:
                    continue

                def body(engine, prog=prog):
                    for waits, fn, inc in prog:
                        for k, v in waits:
                            engine.wait_ge(sems[k], v)
                        if fn is not None:
                            ins = fn(engine)
                            ins.then_inc(sems[inc[0]], inc[1])

                getattr(block, e)(body)


class Ops:
    def __init__(self, S):
        self.S = S

    @staticmethod
    def _ts(*vs):
        return [v.t for v in vs if isinstance(v, Vw)]

    @staticmethod
    def _a(v):
        return v.a if isinstance(v, Vw) else v

    def act(self, out, in_, func, scale=1.0, bias=None, eng="scalar"):
        kw = {"scale": self._a(scale)}
        if bias is not None:
            kw["bias"] = self._a(bias)
        self.S.op(eng, lambda e: e.activation(out=out.a, in_=in_.a, func=func, **kw),
                  self._ts(in_, scale, bias), self._ts(out))

    def tt(self, out, a, b, op, eng="vector"):
        self.S.op(eng, lambda e: e.tensor_tensor(out=out.a, in0=a.a, in1=b.a, op=op), self._ts(a, b), self._ts(out))

    def ts(self, out, a, s1, op0, s2=None, op1=None, eng="vector"):
        if op1 is None:
            fn = lambda e: e.tensor_scalar(out=out.a, in0=a.a, scalar1=self._a(s1), scalar2=None, op0=op0)
        else:
            fn = lambda e: e.tensor_scalar(out=out.a, in0=a.a, scalar1=self._a(s1), scalar2=self._a(s2), op0=op0, op1=op1)
        self.S.op(eng, fn, self._ts(a, s1, s2), self._ts(out))

    def stt(self, out, in0, scalar, in1, op0, op1):
        self.S.op("vector", lambda e: e.scalar_tensor_tensor(out=out.a, in0=in0.a, scalar=self._a(scalar), in1=in1.a,
                                                             op0=op0, op1=op1),
                  self._ts(in0, scalar, in1), self._ts(out))

    def scan(self, out, d0, d1, init, op0=ALU.mult, op1=ALU.add):
        self.S.op("vector", lambda e: e.tensor_tensor_scan(out=out.a, data0=d0.a, data1=d1.a, initial=self._a(init),
                                                           op0=op0, op1=op1),
                  self._ts(d0, d1, init), self._ts(out))

    def cp(self, out, in_, eng="scalar"):
        if eng == "scalar":
            fn = lambda e: e.copy(out=out.a, in_=in_.a)
        else:
            fn = lambda e: e.tensor_copy(out=out.a, in_=in_.a)
        self.S.op(eng, fn, self._ts(in_), self._ts(out))

    def memset(self, out, val, eng="vector"):
        self.S.op(eng, lambda e: e.memset(out.a, val), [], self._ts(out))

    def mm(self, out, lhsT, rhs, start=True, stop=True):
        self.S.op("tensor", lambda e: e.matmul(out.a, lhsT.a, rhs.a, start=start, stop=stop),
                  self._ts(lhsT, rhs), self._ts(out))

    def tr(self, out, in_, ident):
        self.S.op("tensor", lambda e: e.transpose(out.a, in_.a, ident.a), self._ts(in_, ident), self._ts(out))

    def dma(self, out, in_, q="sync", is_out=False, ncd=False):
        if ncd:
            fn = lambda e: e.dma_start(out=out.a, in_=in_.a, allow_slow_non_contiguous=True)
        else:
            fn = lambda e: e.dma_start(out=out.a, in_=in_.a)
        d = self.S.dma(q, fn, self._ts(in_), self._ts(out))
        if is_out:
            self.S.out_deps.append(d)
        return d


class Rot:
    def __init__(self, tiles):
        self.tiles = tiles
        self.i = 0

    def next(self):
        t = self.tiles[self.i % len(self.tiles)]
        self.i += 1
        return t


class _Stop(Exception):
    pass


def build_program(dbg=None, n_groups=5, n_layers=NL, stop=None, glist=None):
    nc = bass.Bass("TRN2", target_bir_lowering=False)
    dbg = dbg or {}

    n_groups_total = len(glist) if glist is not None else n_groups

    def ckpt(name):
        if stop == name:
            raise _Stop()

    def din(name, shape):
        return nc.dram_tensor(name, list(shape), F32, kind="ExternalInput").ap()

    def dout(name, shape):
        return nc.dram_tensor(name, list(shape), F32, kind="ExternalOutput").ap()

    xT_in = din("xT", [D, NT_TOK])
    st_rgconv = din("st_rgconv", [NL, 512, NB, 3])
    st_rgh = din("st_rgh", [NL, 512, NB])
    st_s5re = din("st_s5re", [NL, 128, 16, NB])
    st_s5im = din("st_s5im", [NL, 128, 16, NB])
    st_gdconv = din("st_gdconv", [NL, 1536, NB, 3])
    st_gds = din("st_gds", [NL, NB, 4, 128, 128])
    w_in = din("w_in", [NL, D, NIN])
    rg_w_out = din("rg_w_out", [NL, 512, D])
    s5_glu_w = din("s5_glu_w", [NL, 512, 512])
    s5_w_out = din("s5_w_out", [NL, 512, D])
    gdn_w_out = din("gdn_w_out", [NL, 512, D])
    w_out = din("w_out", [NL, D, D])
    ffn_w_gate = din("ffn_w_gate", [NL, D, DFF])
    ffn_w_up = din("ffn_w_up", [NL, D, DFF])
    ffn_w_down = din("ffn_w_down", [NL, DFF, D])
    pk_in = din("pk", [NL, 128, NPK])
    rgw_in = din("rgw", [NL, 128, 2 * 4 * 128])
    s5b_in = din("s5b", [NL, 128, 5 * 512])
    s5c_in = din("s5c", [NL, 128, 2 * 2048])
    cst_in = din("cst", [128, 128 * 7 + 20])

    def dscr(name, shape):
        return nc.dram_tensor(name, list(shape), BF16, kind="Internal").ap()

    WT = {"w_in": w_in, "rg_w_out": rg_w_out, "s5_glu_w": s5_glu_w, "s5_w_out": s5_w_out, "gdn_w_out": gdn_w_out,
          "w_out": w_out, "ffn_w_gate": ffn_w_gate, "ffn_w_up": ffn_w_up, "ffn_w_down": ffn_w_down}
    NWBLK = 448
    wscr = dscr("wscr", [NWBLK, 128, 2048])

    yT_out = dout("yT", [D, NT_TOK])
    o_p_rgconv = dout("o_p_rgconv", [NL, 512, 3])
    o_p_rgh = dout("o_p_rgh", [NL, 512, 1])
    o_p_s5re = dout("o_p_s5re", [NL, 128, 16])
    o_p_s5im = dout("o_p_s5im", [NL, 128, 16])
    o_p_gdconv = dout("o_p_gdconv", [NL, 1536, 3])
    o_p_gds = dout("o_p_gds", [NL, 4, 128, 128])
    o_s_rgconv = dout("o_s_rgconv", [NL, 512, NB, 3])
    o_s_rgh = dout("o_s_rgh", [NL, 512, NB])
    o_s_s5re = dout("o_s_s5re", [NL, 128, 16, NB])
    o_s_s5im = dout("o_s_s5im", [NL, 128, 16, NB])
    o_s_gdconv = dout("o_s_gdconv", [NL, 1536, NB, 3])
    o_s_gds = dout("o_s_gds", [NL, NB, 4, 128, 128])
    dbg_out = {k: dout("dbg_" + k, shp) for k, shp in dbg.items()}

    with ExitStack() as es:
        S = Sched(nc, es)
        O = Ops(S)
        act, tt, ts, stt, scan, cp, mm, tr, memset, dma = O.act, O.tt, O.ts, O.stt, O.scan, O.cp, O.mm, O.tr, O.memset, O.dma
        F = lambda: S.alloc("F")
        H = lambda: S.alloc("H")
        R = lambda: S.alloc("R")
        C = lambda: S.alloc("C")
        free = S.free
        IOQ = "scalar"
        INV_F32R = False
        rr = lambda v: v
        f32v = (lambda v: Vw(v.t, v.a.bitcast(F32))) if INV_F32R else (lambda v: v)
        QF = (lambda: S.alloc("Q")) if INV_F32R else (lambda: S.alloc("F"))
        S5_POST = "vector"

        def dump(name, vw):
            if name in dbg_out:
                if vw.a.dtype != F32:
                    f_ = F()
                    n_ = vw.a.shape[1]
                    cp(f_[:, 0:n_], vw, eng="vector")
                    dma(dram(dbg_out[name]), f_[:, 0:n_], q="sync", is_out=True, ncd=True)
                    free(f_)
                else:
                    dma(dram(dbg_out[name]), vw, q="sync", is_out=True, ncd=True)

        pA_t = [S.ps([128, 512], F32, "pA") for _ in range(3)]
        pA = Rot(pA_t)
        pB_t = [S.ps([128, 512], F32, "pB") for _ in range(4)]
        pB = Rot(pB_t)
        pT = S.ps([128, 512], F32, "pT")

        cst = S.sb([128, 128 * 7 + 20], F32, "cst")
        dma(cst[:], dram(cst_in), q=IOQ)
        ident_f = cst[:, 0:128]
        maskI_p, maskS_p, maskA_p = cst[:, 128:256], cst[:, 256:384], cst[:, 384:512]
        maskI_s, maskS_s, maskA_s = cst[:, 512:640], cst[:, 640:768], cst[:, 768:896]
        bmask = cst[:, 896:912]
        bandmask = cst[:, 912:916]
        ident_b = S.sb([128, 128], BF16, "identb")
        cp(ident_b[:], ident_f, eng="vector")
        ones_b = S.sb([128, 128], BF16, "onesb")
        memset(ones_b[:], 1.0)
        cvec = S.sb([128, 4], F32, "cvec")
        memset(cvec[:, 0:1], EPS)
        memset(cvec[:, 1:2], 1.0)
        memset(cvec[:, 2:3], float(np.pi / 2))
        memset(cvec[:, 3:4], 0.0)
        eps_c, one_c, hpi_c = cvec[:, 0:1], cvec[:, 1:2], cvec[:, 2:3]
        sel = S.sb([8, 8 * 128], F32, "sel")
        cp(sel[:].re("p (r m) -> p r m", r=8), cst[0:8, 0:8].bc(1, 128), eng="vector")
        resetP = S.sb([128, 512], F32, "resetP")
        memset(resetP[:], 1.0)
        memset(resetP[:].re("p (c t) -> p c t", t=128)[:, :, 0:1], 0.0)
        resetS = S.sb([128, 128], F32, "resetS")
        memset(resetS[:], 1.0)
        memset(resetS[:].re("p (c t) -> p c t", t=8)[:, :, 0:1], 0.0)

        pk, rgw_b, BTre, BTim, CTre, CTimn, cosT, sinT, rhoT, lam_re, lam_im, W_re, W_im = ([None] * NL for _ in range(13))
        c1, nA, wab_b, rho_c, BT3re, BT3im, diagD = ([None] * NL for _ in range(7))
        hist_rg, hist_gd, hlast_rg, rin_re, rin_im, Sp, Sp_b = ([None] * NL for _ in range(7))

        def pkv(l, name, j=None, w=1):
            o, n = PK[name]
            if j is None:
                return pk[l][:, o:o + n]
            return pk[l][:, o + j:o + j + w]

        def csq(c, s, n):
            t1, t2 = F(), F()
            tt(t1[:, 0:n], c, c, ALU.mult)
            tt(t2[:, 0:n], s, s, ALU.mult)
            tt(s, c, s, ALU.mult)
            tt(c, t1[:, 0:n], t2[:, 0:n], ALU.subtract)
            ts(s, s, 2.0, ALU.mult)
            free(t1, t2)

        def cmul(o_re, o_im, a_re, a_im, b_re, b_im, n):
            t1, t2 = F(), F()
            sh = lambda t: Vw(t.t, t.a) if False else t
            v1 = t1[:, 0:n]
            v2 = t2[:, 0:n]
            if len(o_re.a.shape) == 3:
                k = o_re.a.shape[1]
                v1 = v1.re("p (a b) -> p a b", a=k)
                v2 = v2.re("p (a b) -> p a b", a=k)
            tt(v1, a_re, b_re, ALU.mult)
            tt(v2, a_im, b_im, ALU.mult)
            tt(o_re, v1, v2, ALU.subtract)
            tt(v1, a_re, b_im, ALU.mult)
            tt(v2, a_im, b_re, ALU.mult)
            tt(o_im, v1, v2, ALU.add)
            free(t1, t2)

        for l in range(n_layers):
            pk[l] = S.sb([128, NPK], F32, "pk")
            dma(pk[l][:], dram(pk_in[l]), q=IOQ)
            stg = R()
            dma(stg[:, 0:1024], dram(rgw_in[l]), q=IOQ)
            rgw_b[l] = S.sb([128, 1024], BF16, "rgwb")
            cp(rgw_b[l][:], stg[:, 0:1024], eng="vector")
            free(stg)
            c1[l] = S.sb([128, 4], F32, "c1")
            act(c1[l][:], pkv(l, "rg_lam"), AF.Exp, scale=-1.0)
            act(c1[l][:], c1[l][:], AF.Ln, bias=one_c)
            ts(c1[l][:], c1[l][:], -8.0, ALU.mult)
            nA[l] = S.sb([128, 4], F32, "nA")
            act(nA[l][:], pkv(l, "gd_alog"), AF.Exp)
            ts(nA[l][:], nA[l][:], -1.0, ALU.mult)
            diagD[l] = S.sb([128, 512], BF16, "diagD")
            for oc_ in range(4):
                ts(diagD[l][:, 128 * oc_:128 * oc_ + 128], ident_f, pkv(l, "s5_d", oc_), ALU.mult)
            wab_b[l] = S.sb([128, 64], BF16, "wab")
            cp(wab_b[l][:], pkv(l, "w_ab"), eng="vector")

            sv = S.sb([128, 16 * 12], F32, "s5v")
            col = lambda i: sv[:, 16 * i:16 * (i + 1)]
            dt_, ar_, th_, rho_, c_, s_ = col(0), col(1), col(2), col(3), col(4), col(5)
            lam_re[l], lam_im[l], W_re[l], W_im[l] = col(6), col(7), col(8), col(9)
            e_re, e_im = col(10), col(11)
            rho_c[l] = rho_
            act(dt_, pkv(l, "s5_ldt"), AF.Exp)
            tt(ar_, pkv(l, "s5_are"), dt_, ALU.mult)
            tt(th_, pkv(l, "s5_aim"), dt_, ALU.mult)
            act(rho_, ar_, AF.Exp)
            act(s_, th_, AF.Sin, scale=1.0 / 16)
            act(c_, th_, AF.Sin, scale=1.0 / 16, bias=hpi_c)
            for _ in range(4):
                csq(c_, s_, 16)
            tt(lam_re[l], rho_, c_, ALU.mult)
            tt(lam_im[l], rho_, s_, ALU.mult)
            cosT[l] = S.sb([128, 16 * WD_P], F32, "cosT")
            sinT[l] = S.sb([128, 16 * WD_P], F32, "sinT")
            c3 = cosT[l][:].re("p (s t) -> p s t", s=16)
            s3 = sinT[l][:].re("p (s t) -> p s t", s=16)
            memset(c3[:, :, 0:1], 1.0)
            memset(s3[:, :, 0:1], 0.0)
            cp(c3[:, :, 1:2], c_.re("p (s o) -> p s o", o=1), eng="vector")
            cp(s3[:, :, 1:2], s_.re("p (s o) -> p s o", o=1), eng="vector")
            cp(e_re, c_, eng="vector")
            cp(e_im, s_, eng="vector")
            m = 1
            while m < WD_P:
                csq(e_re, e_im, 16)
                m *= 2
                if m < WD_P:
                    cmul(c3[:, :, m:2 * m], s3[:, :, m:2 * m], c3[:, :, 0:m], s3[:, :, 0:m],
                         e_re.bc(1, m), e_im.bc(1, m), 16 * m)
            tt(W_re[l], rho_, e_re, ALU.mult)
            tt(W_im[l], rho_, e_im, ALU.mult)
            rhoT[l] = S.sb([128, 16 * WD_P], F32, "rhoT")
            r3 = rhoT[l][:].re("p (s t) -> p s t", s=16)
            cp(r3, rho_.bc(1, WD_P), eng="vector")
            memset(r3[:, :, 0:1], 0.0)

            bs = [R() for _ in range(3)]
            for i in range(3):
                dma(bs[i][:, 0:1024] if i < 2 else bs[i][:, 0:512], dram(s5b_in[l][:, 1024 * i:1024 * i + (1024 if i < 2 else 512)]), q=IOQ)
            Bre_z, Bim_z = bs[0][:, 0:512], bs[0][:, 512:1024]
            are_b, aim_b = bs[1][:, 0:512], bs[1][:, 512:1024]
            ldt_b = bs[2][:, 0:512]
            wk = [F() for _ in range(8)]
            dtb, arb, thb, rhob, cb, sb_, x_, y_ = [w[:, 0:512] for w in wk]
            act(dtb, ldt_b, AF.Exp)
            tt(arb, are_b, dtb, ALU.mult)
            tt(thb, aim_b, dtb, ALU.mult)
            act(rhob, arb, AF.Exp)
            act(sb_, thb, AF.Sin, scale=1.0 / 16)
            act(cb, thb, AF.Sin, scale=1.0 / 16, bias=hpi_c)
            for _ in range(4):
                csq(cb, sb_, 512)
            tt(x_, rhob, cb, ALU.mult)
            ts(x_, x_, -1.0, ALU.add)
            tt(y_, rhob, sb_, ALU.mult)
            den, sr, si = dtb, arb, thb
            tt(den, are_b, are_b, ALU.mult)
            tt(cb, aim_b, aim_b, ALU.mult)
            tt(den, den, cb, ALU.add)
            S.op("vector", lambda e, o=den.a: e.reciprocal(out=o, in_=o), [den.t], [den.t])
            tt(sr, x_, are_b, ALU.mult)
            tt(cb, y_, aim_b, ALU.mult)
            tt(sr, sr, cb, ALU.add)
            tt(sr, sr, den, ALU.mult)
            tt(si, y_, are_b, ALU.mult)
            tt(cb, x_, aim_b, ALU.mult)
            tt(si, si, cb, ALU.subtract)
            tt(si, si, den, ALU.mult)
            BTre[l] = S.sb([128, 512], BF16, "BTre")
            BTim[l] = S.sb([128, 512], BF16, "BTim")
            tt(x_, sr, Bre_z, ALU.mult)
            tt(y_, si, Bim_z, ALU.mult)
            tt(BTre[l][:], x_, y_, ALU.subtract)
            tt(x_, sr, Bim_z, ALU.mult)
            tt(y_, si, Bre_z, ALU.mult)
            tt(BTim[l][:], x_, y_, ALU.add)
            free(*wk)
            free(*bs)
            CTre[l] = S.sb([128, 2048], BF16, "CTre")
            CTimn[l] = S.sb([128, 2048], BF16, "CTimn")
            stg = R()
            dma(stg[:], dram(s5c_in[l][:, 0:2048]), q=IOQ)
            cp(CTre[l][:], stg[:], eng="vector")
            free(stg)
            stg = R()
            dma(stg[:], dram(s5c_in[l][:, 2048:4096]), q=IOQ)
            ts(CTimn[l][:], stg[:], -1.0, ALU.mult)
            free(stg)

            hist_rg[l] = S.sb([128, 12], F32, "hist_rg")
            hist_gd[l] = S.sb([128, 36], F32, "hist_gd")
            hlast_rg[l] = S.sb([128, 4], F32, "hlast")
            rin_re[l] = S.sb([128, 16], F32, "rin_re")
            rin_im[l] = S.sb([128, 16], F32, "rin_im")
            Sp[l] = S.sb([128, 512], F32, "Sp")
            Sp_b[l] = S.sb([128, 512], BF16, "Spb")
            for t_ in (hist_rg[l], hist_gd[l], hlast_rg[l], rin_re[l], rin_im[l], Sp[l], Sp_b[l]):
                memset(t_[:], 0.0)

        WSZ = 2048
        WBF = Rot([S.sb([128, WSZ], BF16, "wbf") for _ in range(6)])
        cast_rr = [0]

        wblk = {}

        def wload(wname, l_, k0, kn, c0, w):
            Wv = WT[wname][l_].rearrange("(k p) m -> p k m", p=128)
            wb = WBF.next()
            wv_ = wb[:, 0:kn * w].re("p (k m) -> p k m", k=kn)
            key = (wname, l_, c0, k0)
            if key not in wblk:
                dma(wv_, dram(Wv[:, k0:k0 + kn, c0:c0 + w]), q="gpsimd")
                if n_groups_total > 1:
                    bt = Tile(None, "wblk")
                    bid = len(wblk)
                    assert bid < NWBLK
                    wblk[key] = (bt, bid)
                    dma(Vw(bt, wscr[bid, :, 0:kn * w]), wb[:, 0:kn * w], q="sync")
            else:
                bt, bid = wblk[key]
                dma(wb[:, 0:kn * w], Vw(bt, wscr[bid, :, 0:kn * w]), q="sync")
            return wv_

        def dense(W, KC, col0, ncols, rhs, N, consume):
            wname, l_ = W
            if KC * 128 > WSZ:
                npc = -(-KC * 128 // WSZ)
                kp = -(-KC // npc)
                for mi, c0 in enumerate(range(col0, col0 + ncols, 128)):
                    ps = pA.next()
                    for k0 in range(0, KC, kp):
                        kn = min(kp, KC - k0)
                        wv_ = wload(wname, l_, k0, kn, c0, 128)
                        for k in range(kn):
                            mm(ps[:, 0:N], wv_[:, k, :], rhs(k0 + k), start=(k0 + k == 0), stop=(k0 + k == KC - 1))
                    consume(mi, 128, ps)
                return
            bw = min(ncols, 512, max(128, (WSZ // KC) // 128 * 128))
            mi = 0
            for c0 in range(col0, col0 + ncols, bw):
                w = min(bw, col0 + ncols - c0)
                wv_ = wload(wname, l_, 0, KC, c0, w)
                for m0 in range(0, w, 128):
                    msz = min(128, w - m0)
                    ps = pA.next()
                    for k in range(KC):
                        mm(ps[0:msz, 0:N], wv_[:, k, m0:m0 + msz], rhs(k), start=(k == 0), stop=(k == KC - 1))
                    consume(mi, msz, ps)
                    mi += 1

        def rms_rstd(srcs, N, scale):
            ps = pB.next()
            n = len(srcs)
            for k, sv_ in enumerate(srcs):
                sq = H()
                act(sq[:, 0:N], sv_, AF.Square)
                mm(ps[:, 0:N], ones_b[:], sq[:, 0:N], start=(k == 0), stop=(k == n - 1))
                free(sq)
            r = F()
            act(r[:, 0:N], ps[:, 0:N], AF.Ln, scale=scale, bias=eps_c)
            act(r[:, 0:N], r[:, 0:N], AF.Exp, scale=-0.5)
            return r

        try:
          ckpt("prep")
          for gi, (tok0, N, B, T) in [(i, GROUPS[i]) for i in (glist if glist is not None else range(n_groups))]:
              prompt = (B == 1)
              last_p = (gi == 3)
              Wc = T + 3
              xT = [F() for _ in range(KD)]
              xin = xT_in.rearrange("(k p) n -> p k n", p=128)
              for k in range(KD):
                  dma(xT[k][:, 0:N], dram(xin[:, k, tok0:tok0 + N]), q=IOQ)

              for l in range(n_layers):
                  rstd = rms_rstd([xT[k][:, 0:N] for k in range(KD)], N, 1.0 / D)
                  hT = [H() for _ in range(KD)]
                  for k in range(KD):
                      stt(hT[k][:, 0:N], xT[k][:, 0:N], pkv(l, "g_pre", k), rstd[:, 0:N], ALU.mult, ALU.mult)
                  free(rstd)
                  rhs_h = lambda k: hT[k][:, 0:N]

                  ckpt("norm")
                  xcs = [None] * 4

                  if not prompt:
                      cvi_rg, cvo_rg = F(), F()
                      dma(cvi_rg[:, 0:192].re("p (c x) -> p c x", c=4), dram(st_rgconv[l].rearrange("(c p) b k -> p c (b k)", p=128)), q=IOQ)

                  def rg_x(mi, msz, ps):
                      xp = F()
                      x3 = xp[:, 0:B * Wc].re("p (b t) -> p b t", b=B)
                      if prompt:
                          cp(x3[:, 0, 0:3], hist_rg[l][:, 3 * mi:3 * mi + 3])
                      else:
                          cp(x3[:, :, 0:3], cvi_rg[:, 48 * mi:48 * mi + 48].re("p (b k) -> p b k", b=NB), eng="vector")
                      cp(x3[:, :, 3:Wc], ps[:, 0:N].re("p (b t) -> p b t", b=B))
                      if prompt:
                          cp(hist_rg[l][:, 3 * mi:3 * mi + 3], x3[:, 0, T:T + 3], eng="vector")
                          if last_p:
                              dma(dram(o_p_rgconv[l, 128 * mi:128 * mi + 128]), hist_rg[l][:, 3 * mi:3 * mi + 3], q=IOQ, is_out=True, ncd=True)
                      else:
                          cp(cvo_rg[:, 48 * mi:48 * mi + 48].re("p (b k) -> p b k", b=NB), x3[:, :, T:T + 3], eng="vector")
                          if mi == 3:
                              dma(dram(o_s_rgconv[l].rearrange("(c p) b k -> p c (b k)", p=128)), cvo_rg[:, 0:192].re("p (c x) -> p c x", c=4), q=IOQ, is_out=True)
                              free(cvi_rg, cvo_rg)
                      xc = F()
                      xc3 = xc[:, 0:N].re("p (b t) -> p b t", b=B)
                      o4 = PK["rg_cw"][0] + 4 * mi
                      ts(xc3, x3[:, :, 0:T], pk[l][:, o4:o4 + 1], ALU.mult, pkv(l, "rg_cb", mi), ALU.add)
                      for kk in range(1, 4):
                          stt(xc3, x3[:, :, kk:kk + T], pk[l][:, o4 + kk:o4 + kk + 1], xc3, ALU.mult, ALU.add)
                      free(xp)
                      xcs[mi] = xc

                  dense(("w_in", l), KD, C_RGX, 512, rhs_h, N, rg_x)
                  ggs = [None] * 4

                  def rg_g(mi, msz, ps):
                      gg = F()
                      act(gg[:, 0:N], ps[:, 0:N], AF.Gelu_apprx_tanh)
                      ggs[mi] = gg

                  dense(("w_in", l), KD, C_RGG, 512, rhs_h, N, rg_g)
                  y_rg = [None] * 4

                  def rg_gen(c):
                      xc = xcs[c]
                      xcb = H()
                      cp(xcb[:, 0:N], xc[:, 0:N])
                      yield
                      ps_r, ps_i = pB.next(), pB.next()
                      wv = rgw_b[l][:].re("p (a c m) -> p a c m", a=2, c=4)
                      mm(ps_r[:, 0:N], wv[:, 0, c, :], xcb[:, 0:N])
                      mm(ps_i[:, 0:N], wv[:, 1, c, :], xcb[:, 0:N])
                      free(xcb)
                      yield
                      r_, ig, a_ = F(), F(), F()
                      act(r_[:, 0:N], ps_r[:, 0:N], AF.Sigmoid, bias=pkv(l, "rg_ba", c))
                      act(ig[:, 0:N], ps_i[:, 0:N], AF.Sigmoid, bias=pkv(l, "rg_bx", c))
                      yield
                      act(a_[:, 0:N], r_[:, 0:N], AF.Exp, scale=c1[l][:, c:c + 1])
                      tt(ig[:, 0:N], ig[:, 0:N], xc[:, 0:N], ALU.mult)
                      yield
                      tt(r_[:, 0:N], a_[:, 0:N], a_[:, 0:N], ALU.mult)
                      yield
                      act(r_[:, 0:N], r_[:, 0:N], AF.Sqrt, scale=-1.0, bias=one_c)
                      yield
                      tt(ig[:, 0:N], ig[:, 0:N], r_[:, 0:N], ALU.mult)
                      hh = r_
                      if prompt:
                          scan(hh[:, 0:N], a_[:, 0:N], ig[:, 0:N], hlast_rg[l][:, c:c + 1])
                          cp(hlast_rg[l][:, c:c + 1], hh[:, N - 1:N], eng="vector")
                          if last_p:
                              dma(dram(o_p_rgh[l, 128 * c:128 * c + 128]), hlast_rg[l][:, c:c + 1], q=IOQ, is_out=True, ncd=True)
                      else:
                          h0 = C()
                          dma(h0[:, 0:NB], dram(st_rgh[l, 128 * c:128 * c + 128]), q=IOQ)
                          a3 = a_[:, 0:N].re("p (b t) -> p b t", b=B)
                          i3 = ig[:, 0:N].re("p (b t) -> p b t", b=B)
                          t0 = C()
                          tt(t0[:, 0:NB], a3[:, :, 0], h0[:, 0:NB], ALU.mult)
                          tt(i3[:, :, 0], i3[:, :, 0], t0[:, 0:NB], ALU.add)
                          memset(a3[:, :, 0:1], 0.0)
                          free(h0, t0)
                          xc2 = xc
                          scan(xc2[:, 0:N], a_[:, 0:N], ig[:, 0:N], 0.0)
                          hh = xc2
                          hl_ = C()
                          cp(hl_[:, 0:NB], hh[:, 0:N].re("p (b t) -> p b t", b=B)[:, :, T - 1], eng="vector")
                          dma(dram(o_s_rgh[l, 128 * c:128 * c + 128]), hl_[:, 0:NB], q=IOQ, is_out=True)
                          free(hl_)
                      yield
                      y = H()
                      tt(y[:, 0:N], hh[:, 0:N], ggs[c][:, 0:N], ALU.mult)
                      y_rg[c] = y
                      free(r_, ig, a_, xc, ggs[c])

                  for c0_ in (0, 2):
                      gens = [rg_gen(c0_), rg_gen(c0_ + 1)]
                      while gens:
                          for g_ in list(gens):
                              try:
                                  next(g_)
                              except StopIteration:
                                  gens.remove(g_)
                  if l == 0 and gi == 0:
                      for c in range(4):
                          dump("y_rg%d" % c, y_rg[c][:, 0:N])

                  ckpt("rg")
                  uf, ub = [None] * 4, [None] * 4

                  def s5_u(mi, msz, ps):
                      ub[mi] = H()
                      cp(ub[mi][:, 0:N], ps[:, 0:N])

                  dense(("w_in", l), KD, C_S5U, 512, rhs_h, N, s5_u)
                  nch = N // 128
                  RT = (lambda: R()) if prompt else (lambda: F())
                  rv = lambda t_: t_[:, 0:4 * N].re("p (h n) -> p h n", h=4)
                  psr = pB.next()
                  wab3 = wab_b[l][:].re("p (k j) -> p k j", k=8)
                  for k in range(KD):
                      mm(psr[0:8, 0:N], wab3[:, k, :], hT[k][:, 0:N], start=(k == 0), stop=(k == KD - 1))
                  R8 = F()
                  cp(R8[0:8, 0:N], psr[0:8, 0:N])
                  sel3 = sel[:].re("p (r m) -> p r m", r=8)
                  GC, BR, EG = RT(), RT(), RT()
                  for h in range(4):
                      pa = pB.next()
                      mm(pa[:, 0:N], sel3[:, h, :], R8[0:8, 0:N])
                      e1 = F()
                      act(e1[:, 0:N], pa[:, 0:N], AF.Exp, bias=pkv(l, "gd_dtb", h))
                      act(e1[:, 0:N], e1[:, 0:N], AF.Ln, bias=one_c)
                      act(rv(EG)[:, h, :], e1[:, 0:N], AF.Identity, scale=nA[l][:, h:h + 1])
                      free(e1)
                      pb_ = pB.next()
                      mm(pb_[:, 0:N], sel3[:, 4 + h, :], R8[0:8, 0:N])
                      act(rv(BR)[:, h, :], pb_[:, 0:N], AF.Sigmoid)
                  free(R8)
                  ckpt("s5_a")
                  yacc = [F() for _ in range(4)]
                  cols = [None] * nch

                  def col_gen(ci):
                      cs_ = slice(ci * 128, (ci + 1) * 128)
                      mI_, mA_ = (maskI_p, maskA_p) if prompt else (maskI_s, maskA_s)
                      pc = pB.next()
                      for k in range(KD):
                          mm(pc[:, 0:8], hT[k][:, cs_], wab3[:, k, :], start=(k == 0), stop=(k == KD - 1))
                      cc = C()
                      cols[ci] = cc
                      gcol, bcol, gccol, glast, bk, kd = (cc[:, 4 * i:4 * i + 4] for i in range(6))
                      yield
                      tt(gcol, pc[:, 0:4], pkv(l, "gd_dtb"), ALU.add)
                      act(gcol, gcol, AF.Exp)
                      act(gcol, gcol, AF.Ln, bias=one_c)
                      act(bcol, pc[:, 4:8], AF.Sigmoid)
                      yield
                      tt(gcol, gcol, nA[l][:], ALU.mult)
                      pg = pB.next()
                      mm(pg[:, 0:4], mI_, gcol)
                      mm(pg[:, 4:8], mA_, gcol)
                      yield
                      cp(gccol, pg[:, 0:4], eng="vector")
                      tt(glast, pg[:, 4:8], gccol, ALU.subtract)
                      act(kd, glast, AF.Exp)
                      act(bk, gccol, AF.Exp)
                      yield
                      tt(bk, bk, bcol, ALU.mult)

                  col_pending = [col_gen(ci) for ci in range(nch)]

                  def col_step():
                      while col_pending:
                          try:
                              next(col_pending[0])
                              return
                          except StopIteration:
                              col_pending.pop(0)
                  if prompt:
                      Wd, SB = WD_P, 16
                  else:
                      Wd, SB = 128, 4
                  nsub = N // Wd
                  nblk = 16 // SB
                  cT3 = cosT[l][:].re("p (s t) -> p s t", s=16)
                  sT3 = sinT[l][:].re("p (s t) -> p s t", s=16)
                  BTr = BTre[l][:].re("p (u m) -> p u m", u=4)
                  BTi = BTim[l][:].re("p (u m) -> p u m", u=4)
                  BT3r = BT3i = None
                  CTr = CTre[l][:].re("p (s m) -> p s m", s=16)
                  CTi = CTimn[l][:].re("p (s m) -> p s m", s=16)

                  ubq = [[None] * 4 for _ in range(4)]
                  for uc_ in range(4):
                      for q_ in range(4):
                          ubq[uc_][q_] = H()
                          act(ubq[uc_][q_][:, 0:N], ub[uc_][:, 0:N], AF.Identity, scale=bandmask[:, q_:q_ + 1])

                  def bmm(out, BT_, BT3_, sc, cols):
                      uc, q = sc // 4, sc % 4
                      mm(out, BT_[:, uc, :], ubq[uc][q][:, cols])

                  if not prompt:
                      s0r, s0i, rsr, rsi, fin_r, fin_i = F(), F(), F(), F(), F(), F()
                      dma(s0r[:, 0:256], dram(st_s5re[l].rearrange("p s b -> p (s b)")), q=IOQ)
                      dma(s0i[:, 0:256], dram(st_s5im[l].rearrange("p s b -> p (s b)")), q=IOQ)
                      v3 = lambda t_: t_[:, 0:256].re("p (s b) -> p s b", s=16)
                      cmul(v3(rsr), v3(rsi), v3(s0r), v3(s0i), lam_re[l].bc(1, NB), lam_im[l].bc(1, NB), 256)
                      free(s0r, s0i)
                      rhoS = R()
                      cp(rhoS[:].re("p (s t) -> p s t", s=16), rho_c[l].bc(1, 128), eng="vector")
                      memset(rhoS[:].re("p (c t) -> p c t", t=8)[:, :, 0:1], 0.0)
                  pre, pim = pA_t[0], pA_t[1]
                  s1, s2 = pA_t[2], pT
                  units = [(j, sb_i) for j in range(nsub) for sb_i in range(nblk)]

                  def emitB(u_):
                      j_, sb_ = units[u_]
                      for s_ in range(SB):
                          bmm(pre[:, s_ * Wd:(s_ + 1) * Wd], BTr, BT3r, sb_ * SB + s_, slice(j_ * Wd, j_ * Wd + Wd))
                      for s_ in range(SB):
                          bmm(pim[:, s_ * Wd:(s_ + 1) * Wd], BTi, BT3i, sb_ * SB + s_, slice(j_ * Wd, j_ * Wd + Wd))

                  emitB(0)
                  for ui, (j, sb_i) in enumerate(units):
                      if True:
                          sc0 = sb_i * SB
                          t0_ = j * Wd
                          if prompt:
                              dv = lambda t_: t_[:, 0:512].re("p (s t) -> p s t", s=SB)
                              cV, sV = cT3[:, sc0:sc0 + SB, 0:Wd], sT3[:, sc0:sc0 + SB, 0:Wd]
                              rV = rhoT[l][:, sc0 * Wd:(sc0 + SB) * Wd]
                          else:
                              dv = lambda t_: t_[:, 0:512].re("p (s b t) -> p s b t", s=SB, b=NB)
                              cV, sV = cT3[:, sc0:sc0 + SB, 0:8].bc(1, NB), sT3[:, sc0:sc0 + SB, 0:8].bc(1, NB)
                              rV = rhoS[:, sc0 * 128:(sc0 + SB) * 128]
                          ckpt("s5_b")
                          t1, t2, zr, zi = F(), F(), F(), F()
                          tt(dv(t1), dv(pre), cV, ALU.mult)
                          tt(dv(t2), dv(pim), sV, ALU.mult)
                          tt(zr[:, 0:512], t1[:, 0:512], t2[:, 0:512], ALU.add)
                          tt(dv(t1), dv(pim), cV, ALU.mult)
                          tt(dv(t2), dv(pre), sV, ALU.mult)
                          tt(zi[:, 0:512], t1[:, 0:512], t2[:, 0:512], ALU.subtract)
                          ckpt("s5_c")
                          if ui + 1 < len(units):
                              emitB(ui + 1)
                          col_step()
                          if prompt:
                              zr0 = dv(zr)[:, :, 0]
                              zi0 = dv(zi)[:, :, 0]
                              tt(zr0, zr0, rin_re[l][:, sc0:sc0 + SB], ALU.add)
                              tt(zi0, zi0, rin_im[l][:, sc0:sc0 + SB], ALU.add)
                          else:
                              zr0 = dv(zr)[:, :, :, 0]
                              zi0 = dv(zi)[:, :, :, 0]
                              tt(zr0, zr0, v3(rsr)[:, sc0:sc0 + SB, :], ALU.add)
                              tt(zi0, zi0, v3(rsi)[:, sc0:sc0 + SB, :], ALU.add)
                          ckpt("s5_d")
                          scan(s1[:, 0:512], rV, zr[:, 0:512], 0.0)
                          scan(s2[:, 0:512], rV, zi[:, 0:512], 0.0)
                          ckpt("s5_e")
                          ore, oim = H(), H()
                          p1, p2 = t1, t2
                          tt(dv(p1), dv(s1), cV, ALU.mult, eng=S5_POST)
                          tt(dv(p2), dv(s2), sV, ALU.mult, eng=S5_POST)
                          tt(ore[:], p1[:, 0:512], p2[:, 0:512], ALU.subtract, eng=S5_POST)
                          tt(dv(p1), dv(s1), sV, ALU.mult, eng=S5_POST)
                          tt(dv(p2), dv(s2), cV, ALU.mult, eng=S5_POST)
                          tt(oim[:], p1[:, 0:512], p2[:, 0:512], ALU.add, eng=S5_POST)
                          ckpt("s5_f")
                          if prompt:
                              slr, sli = dv(s1)[:, :, Wd - 1], dv(s2)[:, :, Wd - 1]
                              if last_p and j == nsub - 1:
                                  fr, fi = C(), C()
                                  cmul(fr[:, 0:16], fi[:, 0:16], slr, sli, cT3[:, :, Wd - 1], sT3[:, :, Wd - 1], 16)
                                  dma(dram(o_p_s5re[l]), fr[:, 0:16], q=IOQ, is_out=True)
                                  dma(dram(o_p_s5im[l]), fi[:, 0:16], q=IOQ, is_out=True)
                                  free(fr, fi)
                              cmul(rin_re[l][:], rin_im[l][:], slr, sli, W_re[l], W_im[l], 16)
                          else:
                              slr, sli = dv(s1)[:, :, :, T - 1], dv(s2)[:, :, :, T - 1]
                              cmul(v3(fin_r)[:, sc0:sc0 + SB, :], v3(fin_i)[:, sc0:sc0 + SB, :], slr, sli,
                                   cT3[:, sc0:sc0 + SB, T - 1].bc(1, NB), sT3[:, sc0:sc0 + SB, T - 1].bc(1, NB), SB * NB)
                          free(t1, t2, zr, zi)
                          ckpt("s5_g")
                          psY = pB.next()
                          for s_ in range(SB):
                              sc = sc0 + s_
                              oc = sc // 4
                              c0_ = oc * Wd if prompt else 0
                              mm(psY[:, c0_:c0_ + Wd], CTr[:, sc, :], ore[:, s_ * Wd:(s_ + 1) * Wd], start=(sc % 4 == 0), stop=False)
                              mm(psY[:, c0_:c0_ + Wd], CTi[:, sc, :], oim[:, s_ * Wd:(s_ + 1) * Wd], start=False, stop=False)
                              if sc % 4 == 3:
                                  mm(psY[:, c0_:c0_ + Wd], diagD[l][:, 128 * oc:128 * oc + 128], ub[oc][:, t0_:t0_ + Wd], start=False, stop=True)
                          for oc in sorted(set((sc0 + s_) // 4 for s_ in range(SB))):
                              c0_ = oc * Wd if prompt else 0
                              cp(yacc[oc][:, t0_:t0_ + Wd], psY[:, c0_:c0_ + Wd])
                          free(ore, oim)
                          ckpt("s5_h")
                  if not prompt:
                      dma(dram(o_s_s5re[l].rearrange("p s b -> p (s b)")), fin_r[:, 0:256], q=IOQ, is_out=True)
                      dma(dram(o_s_s5im[l].rearrange("p s b -> p (s b)")), fin_i[:, 0:256], q=IOQ, is_out=True)
                      free(rsr, rsi, fin_r, fin_i, rhoS)
                  for uc_ in range(4):
                      free(*ubq[uc_])
                  while col_pending:
                      col_step()
                  ygs, ygb = [None] * 4, [None] * 4
                  for oc in range(4):
                      yv = yacc[oc]
                      act(yv[:, 0:N], yv[:, 0:N], AF.Gelu_apprx_tanh)
                      ygb[oc] = H()
                      cp(ygb[oc][:, 0:N], yv[:, 0:N], eng="vector")
                      ygs[oc] = yv
                      free(ub[oc])
                  y_s5 = [None] * 4

                  def s5_glu(mi, msz, ps):
                      sg = F()
                      act(sg[:, 0:N], ps[:, 0:N], AF.Sigmoid, bias=pkv(l, "s5_glub", mi))
                      y_s5[mi] = H()
                      tt(y_s5[mi][:, 0:N], ygs[mi][:, 0:N], sg[:, 0:N], ALU.mult)
                      free(sg, ygs[mi])

                  dense(("s5_glu_w", l), 4, 0, 512, lambda k: ygb[k][:, 0:N], N, s5_glu)
                  free(*ygb)
                  if l == 0 and gi == 0:
                      for c in range(4):
                          dump("y_s5%d" % c, y_s5[c][:, 0:N])

                  ckpt("s5")
                  rmask = resetP[:, 0:N] if prompt else resetS[:, 0:N]
                  for h in range(4):
                      scan(rv(GC)[:, h, :], rmask, rv(EG)[:, h, :], 0.0)
                  act(EG[:, 0:4 * N], GC[:, 0:4 * N], AF.Exp)

                  if not prompt:
                      cvi_gd, cvo_gd = [F(), F()], [F(), F()]
                      for hf_ in range(2):
                          dma(cvi_gd[hf_][:, 0:288].re("p (c x) -> p c x", c=6),
                              dram(st_gdconv[l, 768 * hf_:768 * hf_ + 768].rearrange("(c p) b k -> p c (b k)", p=128)), q=IOQ)

                  def gd_x(mi, msz, ps):
                      xp = F()
                      x3 = xp[:, 0:B * Wc].re("p (b t) -> p b t", b=B)
                      if prompt:
                          cp(x3[:, 0, 0:3], hist_gd[l][:, 3 * mi:3 * mi + 3])
                      else:
                          cp(x3[:, :, 0:3], cvi_gd[mi // 6][:, 48 * (mi % 6):48 * (mi % 6) + 48].re("p (b k) -> p b k", b=NB), eng="vector")
                      cp(x3[:, :, 3:Wc], ps[:, 0:N].re("p (b t) -> p b t", b=B))
                      if prompt:
                          cp(hist_gd[l][:, 3 * mi:3 * mi + 3], x3[:, 0, T:T + 3], eng="vector")
                          if last_p:
                              dma(dram(o_p_gdconv[l, 128 * mi:128 * mi + 128]), hist_gd[l][:, 3 * mi:3 * mi + 3], q=IOQ, is_out=True, ncd=True)
                      else:
                          cp(cvo_gd[mi // 6][:, 48 * (mi % 6):48 * (mi % 6) + 48].re("p (b k) -> p b k", b=NB), x3[:, :, T:T + 3], eng="vector")
                          if mi % 6 == 5:
                              hf_ = mi // 6
                              dma(dram(o_s_gdconv[l, 768 * hf_:768 * hf_ + 768].rearrange("(c p) b k -> p c (b k)", p=128)),
                                  cvo_gd[hf_][:, 0:288].re("p (c x) -> p c x", c=6), q=IOQ, is_out=True)
                              free(cvi_gd[hf_], cvo_gd[hf_])
                      xc = F()
                      xc3 = xc[:, 0:N].re("p (b t) -> p b t", b=B)
                      o4 = PK["gd_cw"][0] + 4 * mi
                      ts(xc3, x3[:, :, 0:T], pk[l][:, o4:o4 + 1], ALU.mult)
                      for kk in range(1, 4):
                          stt(xc3, x3[:, :, kk:kk + T], pk[l][:, o4 + kk:o4 + kk + 1], xc3, ALU.mult, ALU.add)
                      free(xp)
                      act(xc[:, 0:N], xc[:, 0:N], AF.Silu)
                      prev = list(gd_pending)
                      del gd_pending[:]
                      if mi < 8:
                          sq = H()
                          act(sq[:, 0:N], xc[:, 0:N], AF.Square)

                          def fin(mi=mi, xc=xc, sq=sq):
                              pn = pB.next()
                              mm(pn[:, 0:N], ones_b[:], sq[:, 0:N])
                              free(sq)
                              rq = F()
                              act(rq[:, 0:N], pn[:, 0:N], AF.Ln, scale=1.0, bias=eps_c)
                              act(rq[:, 0:N], rq[:, 0:N], AF.Exp, scale=-0.5)
                              tt(xc[:, 0:N], xc[:, 0:N], rq[:, 0:N], ALU.mult)
                              free(rq)
                              if mi < 4:
                                  qb[mi], qgb[mi] = H(), H()
                                  ts(qb[mi][:, 0:N], xc[:, 0:N], 128.0 ** -0.5, ALU.mult)
                                  stt(qgb[mi][:, 0:N], xc[:, 0:N], 128.0 ** -0.5, rv(EG)[:, mi, :], ALU.mult, ALU.mult)
                                  free(xc)
                              else:
                                  kf[mi - 4] = xc
                                  kb16[mi - 4] = H()
                                  cp(kb16[mi - 4][:, 0:N], xc[:, 0:N])

                          gd_pending.append(fin)
                      else:
                          vf[mi - 8] = xc
                      for f_ in prev:
                          f_()

                  qb, qgb, kb16, kf, vf = [None] * 4, [None] * 4, [None] * 4, [None] * 4, [None] * 4
                  gd_pending = []
                  dense(("w_in", l), KD, C_QKV, 1536, rhs_h, N, gd_x)
                  for f_ in gd_pending:
                      f_()
                  EGl = C()
                  if prompt:
                      egl3 = EGl[:, 0:4 * nch].re("p (h c) -> p h c", h=4)
                      cp(egl3, rv(EG).re("p h (c t) -> p h c t", t=128)[:, :, :, 127], eng="vector")
                  else:
                      egl3 = EGl[:, 0:4 * NB].re("p (h b) -> p h b", h=4)
                      cp(egl3, rv(EG).re("p h (b t) -> p h b t", t=8)[:, :, :, 7], eng="vector")
                  free(EG)
                  OR = RT()
                  mI, mS, mA = (maskI_p, maskS_p, maskA_p) if prompt else (maskI_s, maskS_s, maskA_s)
                  nsteps = 6 if prompt else 2
                  v4 = lambda t_: t_[:, 0:512].re("p (h m) -> p h m", h=4)
                  decs = [None] * nch

                  def dec_gen(ci_):
                      cs_ = slice(ci_ * 128, (ci_ + 1) * 128)
                      gccol_ = cols[ci_][:, 8:12]
                      Dm_, E1_ = F(), F()
                      decs[ci_] = (Dm_, E1_)
                      tt(v4(Dm_), rv(GC)[:, :, cs_], gccol_.bc(1, 128), ALU.subtract)
                      ts(Dm_[:, 0:512], Dm_[:, 0:512], 0.0, ALU.min)
                      yield
                      act(Dm_[:, 0:512], Dm_[:, 0:512], AF.Exp)
                      yield
                      tt(v4(E1_), v4(Dm_), mS.bc(0, 4), ALU.mult)
                      yield
                      tt(v4(E1_), v4(E1_), rv(BR)[:, :, cs_], ALU.mult)
                      tt(v4(Dm_), v4(Dm_), mI.bc(0, 4), ALU.mult)

                  for _ in dec_gen(0):
                      pass
                  for ci in range(nch):
                      cs = slice(ci * 128, (ci + 1) * 128)
                      dec_next = dec_gen(ci + 1) if ci + 1 < nch else iter(())
                      cc = cols[ci]
                      gcol, bcol, gccol, glast, bk, kd = (cc[:, 4 * i:4 * i + 4] for i in range(6))
                      pk_, pv_ = pB.next(), pB.next()
                      for h in range(4):
                          tr(pk_[:, 128 * h:128 * h + 128], kf[h][:, cs], ident_f)
                      for h in range(4):
                          tr(pv_[:, 128 * h:128 * h + 128], vf[h][:, cs], ident_f)
                      vb, kbg, kdec = H(), H(), H()
                      tt(v4(vb), v4(pv_), bcol.bc(1, 128), ALU.mult)
                      tt(v4(kbg), v4(pk_), bk.bc(1, 128), ALU.mult)
                      tt(v4(kdec), v4(pk_), kd.bc(1, 128), ALU.mult)
                      pkk, pqk = pB.next(), pB.next()
                      for h in range(4):
                          mm(pkk[:, 128 * h:128 * h + 128], kb16[h][:, cs], kb16[h][:, cs])
                      for h in range(4):
                          mm(pqk[:, 128 * h:128 * h + 128], kb16[h][:, cs], qb[h][:, cs])
                      Dm, E1 = decs[ci]
                      Mf, QKm = QF(), H()
                      tt(rr(Mf[:, 0:512]), pkk[:], E1[:, 0:512], ALU.mult)
                      tt(QKm[:], pqk[:], Dm[:, 0:512], ALU.mult)
                      free(Dm, E1)
                      pL = pB.next()
                      for h in range(4):
                          tr(pL[:, 128 * h:128 * h + 128], f32v(Mf[:, 128 * h:128 * h + 128]), ident_f)
                      P_ = QF()
                      cp(rr(P_[:, 0:512]), pL[:])
                      Xf = QF()
                      tt(v4(Xf), ident_f.bc(0, 4), f32v(v4(Mf)), ALU.subtract)
                      Q_ = Mf
                      for st in range(1, nsteps + 1):
                          next(dec_next, None)
                          pP = pB.next()
                          for h in range(4):
                              hs = slice(128 * h, 128 * h + 128)
                              mm(pP[:, hs], rr(Q_[:, hs]), rr(P_[:, hs]))
                          nP = QF()
                          cp(rr(nP[:, 0:512]), pP[:])
                          nQ = None
                          if st < nsteps:
                              pQ = pB.next()
                              for h in range(4):
                                  hs = slice(128 * h, 128 * h + 128)
                                  mm(pQ[:, hs], rr(P_[:, hs]), rr(Q_[:, hs]))
                              nQ = QF()
                              cp(rr(nQ[:, 0:512]), pQ[:], eng="vector")
                          pX = pB.next()
                          for h in range(4):
                              hs = slice(128 * h, 128 * h + 128)
                              mm(pX[:, hs], rr(nP[:, hs]), rr(Xf[:, hs]))
                          nX = QF()
                          tt(nX[:, 0:512], pX[:], f32v(Xf[:, 0:512]), ALU.add)
                          free(P_, Q_, Xf)
                          P_, Q_, Xf = nP, nQ, nX
                      free(P_)
                      for _ in dec_next:
                          pass
                      X = H()
                      cp(X[:], f32v(Xf[:, 0:512]))
                      free(Xf)
                      pw = pB.next()
                      for h in range(4):
                          hs = slice(128 * h, 128 * h + 128)
                          mm(pw[:, hs], kbg[:, hs], X[:, hs])
                      nwT = H()
                      act(nwT[:], pw[:], AF.Copy, scale=-1.0)
                      free(kbg)
                      if prompt:
                          pvn = pB.next()
                          for h in range(4):
                              hs = slice(128 * h, 128 * h + 128)
                              mm(pvn[:, hs], X[:, hs], vb[:, hs], start=True, stop=False)
                              mm(pvn[:, hs], nwT[:, hs], Sp_b[l][:, hs], start=False, stop=True)
                          free(vb, nwT, X)
                          vn = H()
                          cp(vn[:], pvn[:])
                          po = pB.next()
                          for h in range(4):
                              hs = slice(128 * h, 128 * h + 128)
                              mm(po[:, hs], vn[:, hs], QKm[:, hs], start=True, stop=False)
                              mm(po[:, hs], Sp_b[l][:, hs], qgb[h][:, cs], start=False, stop=True)
                          free(QKm)
                          cp(rv(OR)[:, :, cs], v4(po))
                          pS = pB.next()
                          for h in range(4):
                              hs = slice(128 * h, 128 * h + 128)
                              mm(pS[:, hs], kdec[:, hs], vn[:, hs])
                          tt(v4(Sp[l]), v4(Sp[l]), egl3[:, :, ci].bc(1, 128), ALU.mult)
                          tt(Sp[l][:], Sp[l][:], pS[:], ALU.add)
                          cp(Sp_b[l][:], Sp[l][:])
                          if last_p and ci == nch - 1:
                              dma(dram(o_p_gds[l].rearrange("h k v -> k h v")), v4(Sp[l]), q=IOQ, is_out=True)
                          free(vn)
                      else:
                          for h in range(4):
                              hs = slice(128 * h, 128 * h + 128)
                              SR = R()
                              SR3 = SR[:].re("p (b v) -> p b v", b=NB)
                              dma(SR3, dram(st_gds[l, :, h].rearrange("b k v -> k b v")), q=IOQ)
                              Sb = [H() for _ in range(4)]
                              for bq in range(4):
                                  cp(Sb[bq][:], SR[:, 512 * bq:512 * bq + 512])
                              wz = [H() for _ in range(4)]
                              for bq in range(4):
                                  memset(wz[bq][:], 0.0)
                                  o_ap = Vw(wz[bq], bass.AP(wz[bq].h[:].tensor, wz[bq].h[:].offset + 32 * bq,
                                                            [list(wz[bq].h[:].ap[0]), [136, 4], [1, 8]]))
                                  i_ap = nwT[:, 128 * h + 32 * bq:128 * h + 32 * bq + 32].re("p (b t) -> p b t", b=4)
                                  cp(o_ap, i_ap, eng="vector")
                              pvn = pB.next()
                              mm(pvn[:, 0:128], X[:, hs], vb[:, hs], start=True, stop=False)
                              for b in range(NB):
                                  bs_ = slice(128 * (b % 4), 128 * (b % 4) + 128)
                                  mm(pvn[:, 0:128], wz[b // 4][:, bs_], Sb[b // 4][:, bs_], start=False, stop=(b == NB - 1))
                              free(*wz)
                              vn = H()
                              cp(vn[:, 0:128], pvn[:, 0:128])
                              po = pB.next()
                              mm(po[:, 0:128], vn[:, 0:128], QKm[:, hs], start=True, stop=False)
                              for b in range(NB):
                                  bs_ = slice(128 * (b % 4), 128 * (b % 4) + 128)
                                  mm(po[:, 8 * b:8 * b + 8], Sb[b // 4][:, bs_], qgb[h][:, 8 * b:8 * b + 8], start=False, stop=(b == NB - 1))
                              cp(rv(OR)[:, h, cs], po[:, 0:128])
                              for bq in range(4):
                                  kz4 = H()
                                  tt(v4(kz4), kdec[:, hs].bc(0, 4), bmask[:, 4 * bq:4 * bq + 4].bc(1, 128), ALU.mult)
                                  pS = pB.next()
                                  for bl in range(4):
                                      mm(pS[:, 128 * bl:128 * bl + 128], kz4[:, 128 * bl:128 * bl + 128], vn[:, 0:128])
                                  free(kz4)
                                  srv = SR[:, 512 * bq:512 * bq + 512]
                                  srv4 = srv.re("p (b v) -> p b v", b=4)
                                  tt(srv4, srv4, egl3[:, h, 4 * bq:4 * bq + 4].bc(1, 128), ALU.mult)
                                  tt(srv, srv, pS[:], ALU.add)
                              dma(dram(o_s_gds[l, :, h].rearrange("b k v -> k b v")), SR3, q=IOQ, is_out=True)
                              free(SR, vn, *Sb)
                          free(vb, nwT, X, QKm)
                      free(kdec, cc)
                  free(GC, BR, EGl)
                  free(*kf)
                  free(*vf)
                  free(*qb)
                  free(*qgb)
                  free(*kb16)
                  sqs_, pss_, rsd = [], [], []
                  for h in range(4):
                      sq = H()
                      act(sq[:, 0:N], rv(OR)[:, h, :], AF.Square)
                      sqs_.append(sq)
                  for h in range(4):
                      pn = pB.next()
                      mm(pn[:, 0:N], ones_b[:], sqs_[h][:, 0:N])
                      pss_.append(pn)
                  free(*sqs_)
                  for h in range(4):
                      r_ = F()
                      act(r_[:, 0:N], pss_[h][:, 0:N], AF.Ln, scale=1.0 / 128, bias=eps_c)
                      rsd.append(r_)
                  for h in range(4):
                      act(rsd[h][:, 0:N], rsd[h][:, 0:N], AF.Exp, scale=-0.5)
                  gdn_o = [None] * 4

                  def gd_z(mi, msz, ps):
                      zs = F()
                      act(zs[:, 0:N], ps[:, 0:N], AF.Silu)
                      tt(rsd[mi][:, 0:N], rsd[mi][:, 0:N], rv(OR)[:, mi, :], ALU.mult)
                      gdn_o[mi] = H()
                      stt(gdn_o[mi][:, 0:N], rsd[mi][:, 0:N], pkv(l, "gd_nw"), zs[:, 0:N], ALU.mult, ALU.mult)
                      free(zs, rsd[mi])

                  dense(("w_in", l), KD, C_Z, 512, rhs_h, N, gd_z)
                  free(OR)
                  if l == 0 and gi == 0:
                      for c in range(4):
                          dump("gdn_o%d" % c, gdn_o[c][:, 0:N])

                  ckpt("gdn")
                  merged = [None] * KD
                  for mp in range(0, KD, 2):
                      macc = [F(), F()]
                      for bi, (Wb, yb) in enumerate((("rg_w_out", y_rg), ("s5_w_out", y_s5), ("gdn_w_out", gdn_o))):
                          sg = [None, None]

                          def mg_gate(j_, msz, ps):
                              sg[j_] = F()
                              act(sg[j_][:, 0:N], ps[:, 0:N], AF.Sigmoid)

                          dense(("w_in", l), KD, C_GATE + D * bi + 128 * mp, 256, rhs_h, N, mg_gate)

                          def mg_y(j_, msz, ps):
                              if bi == 0:
                                  tt(macc[j_][:, 0:N], ps[:, 0:N], sg[j_][:, 0:N], ALU.mult)
                              else:
                                  tt(sg[j_][:, 0:N], ps[:, 0:N], sg[j_][:, 0:N], ALU.mult)
                                  if bi == 1:
                                      tt(macc[j_][:, 0:N], macc[j_][:, 0:N], sg[j_][:, 0:N], ALU.add)
                                  else:
                                      merged[mp + j_] = H()
                                      tt(merged[mp + j_][:, 0:N], macc[j_][:, 0:N], sg[j_][:, 0:N], ALU.add)
                              free(sg[j_])

                          dense((Wb, l), 4, 128 * mp, 256, lambda k: yb[k][:, 0:N], N, mg_y)
                      free(*macc)
                  free(*y_rg)
                  free(*y_s5)
                  free(*gdn_o)
                  free(*hT)
                  mixed = [None] * KD

                  def mx(mi, msz, ps):
                      mixed[mi] = F()
                      cp(mixed[mi][:, 0:N], ps[:, 0:N])

                  dense(("w_out", l), KD, 0, D, lambda k: merged[k][:, 0:N], N, mx)
                  free(*merged)
                  rstd = rms_rstd([mixed[k][:, 0:N] for k in range(KD)], N, 1.0 / D)
                  for k in range(KD):
                      stt(mixed[k][:, 0:N], mixed[k][:, 0:N], pkv(l, "g_post", k), rstd[:, 0:N], ALU.mult, ALU.mult)
                      tt(xT[k][:, 0:N], xT[k][:, 0:N], mixed[k][:, 0:N], ALU.add)
                  free(rstd)
                  free(*mixed)
                  if l == 0 and gi == 0:
                      for k in range(KD):
                          dump("xmix%d" % k, xT[k][:, 0:N])

                  ckpt("merge")
                  rstd = rms_rstd([xT[k][:, 0:N] for k in range(KD)], N, 1.0 / D)
                  hf = [H() for _ in range(KD)]
                  for k in range(KD):
                      stt(hf[k][:, 0:N], xT[k][:, 0:N], pkv(l, "g_pref", k), rstd[:, 0:N], ALU.mult, ALU.mult)
                  free(rstd)
                  rhs_f = lambda k: hf[k][:, 0:N]
                  acts = [None] * KF
                  for f0 in range(0, KF, 2):
                      sg = {}

                      def ff_g(mi, msz, ps):
                          sg[mi] = F()
                          act(sg[mi][:, 0:N], ps[:, 0:N], AF.Silu)

                      def ff_u(mi, msz, ps):
                          acts[f0 + mi] = H()
                          tt(acts[f0 + mi][:, 0:N], ps[:, 0:N], sg[mi][:, 0:N], ALU.mult)
                          free(sg[mi])

                      dense(("ffn_w_gate", l), KD, 128 * f0, 256, rhs_f, N, ff_g)
                      dense(("ffn_w_up", l), KD, 128 * f0, 256, rhs_f, N, ff_u)
                  free(*hf)
                  fo = [None] * KD

                  def ff_d(mi, msz, ps):
                      fo[mi] = F()
                      cp(fo[mi][:, 0:N], ps[:, 0:N])

                  dense(("ffn_w_down", l), KF, 0, D, lambda k: acts[k][:, 0:N], N, ff_d)
                  free(*acts)
                  rstd = rms_rstd([fo[k][:, 0:N] for k in range(KD)], N, 1.0 / D)
                  for k in range(KD):
                      stt(fo[k][:, 0:N], fo[k][:, 0:N], pkv(l, "g_postf", k), rstd[:, 0:N], ALU.mult, ALU.mult)
                      tt(xT[k][:, 0:N], xT[k][:, 0:N], fo[k][:, 0:N], ALU.add)
                  free(rstd)
                  free(*fo)
                  if l == 0 and gi == 0:
                      for k in range(KD):
                          dump("xl0_%d" % k, xT[k][:, 0:N])

              yo = yT_out.rearrange("(k p) n -> p k n", p=128)
              for k in range(KD):
                  dma(dram(yo[:, k, tok0:tok0 + N]), xT[k][:, 0:N], q=IOQ, is_out=True)
              free(*xT)

        except _Stop:
            pass
        S.wait_all("sync", S.out_deps)
        S.emit()
        info = {"sbuf_bytes": S.sb_bytes, "counts": {e: S.count[e] for e in S.ENGS}, "pools": S.pool_n}
    return nc, info


def _consts():
    c = np.zeros((128, 128 * 7 + 20), np.float32)
    j = np.arange(128)[:, None]
    i = np.arange(128)[None, :]
    c[:, 0:128] = (j == i)
    c[:, 128:256] = (j <= i)
    c[:, 256:384] = (j < i)
    c[:, 384:512] = 1.0
    same = (j // 8) == (i // 8)
    c[:, 512:640] = same & (j <= i)
    c[:, 640:768] = same & (j < i)
    c[:, 768:896] = same
    c[:, 896:912] = (j // 8) == np.arange(16)[None, :]
    c[:, 912:916] = (j // 32) == np.arange(4)[None, :]
    return c


def _fm(v):
    return np.ascontiguousarray(v.reshape(-1, 128).T)


def _pack_layer(l, p):
    pk = np.zeros((128, NPK), np.float32)

    def put(name, arr):
        o, n = PK[name]
        pk[:, o:o + n] = arr.reshape(128, n)

    put("g_pre", _fm(p["norm_pre_mix"][l]))
    put("g_post", _fm(p["norm_post_mix"][l]))
    put("g_pref", _fm(p["norm_pre_ffn"][l]))
    put("g_postf", _fm(p["norm_post_ffn"][l]))
    put("rg_cw", p["rg_conv_w"][l].reshape(4, 4, 128).transpose(2, 1, 0))
    put("rg_cb", _fm(p["rg_conv_b"][l]))
    put("rg_ba", _fm(p["rg_ba"][l]))
    put("rg_bx", _fm(p["rg_bx"][l]))
    put("rg_lam", _fm(p["rg_lambda"][l]))
    put("s5_d", _fm(p["s5_d"][l]))
    put("s5_glub", _fm(p["s5_glu_b"][l]))
    put("gd_cw", p["gdn_conv_w"][l].reshape(4, 12, 128).transpose(2, 1, 0))
    put("gd_nw", p["gdn_norm_w"][l].reshape(128, 1))
    st = lambda a: a.reshape(16, 2, 64).transpose(1, 2, 0).reshape(128, 16)
    put("s5_are", st(p["s5_a_re"][l]))
    put("s5_aim", st(p["s5_a_im"][l]))
    put("s5_ldt", st(np.repeat(p["s5_log_dt"][l][:, None], 64, axis=1)))
    put("gd_alog", np.repeat(p["gdn_a_log"][l][None, :], 128, axis=0))
    put("gd_dtb", np.repeat(p["gdn_dt_bias"][l][None, :], 128, axis=0))
    put("w_ab", p["w_in"][l][:, C_AB:C_AB + 8].reshape(8, 128, 8).transpose(1, 0, 2))
    rgw = np.zeros((128, 2, 4, 128), np.float32)
    for a, w in enumerate((p["rg_wa"][l], p["rg_wx"][l])):
        for c in range(4):
            for n2 in range(2):
                rgw[n2 * 64:(n2 + 1) * 64, a, c, n2 * 64:(n2 + 1) * 64] = w[2 * c + n2]
    s5b = np.zeros((128, 5, 4, 128), np.float32)
    are, aim, ldt = p["s5_a_re"][l], p["s5_a_im"][l], p["s5_log_dt"][l]
    for sc in range(16):
        uc, q = sc // 4, sc % 4
        for g2 in range(2):
            g = 2 * sc + g2
            r0 = q * 32 + g2 * 16
            s5b[r0:r0 + 16, 0, uc, g2 * 64:(g2 + 1) * 64] = p["s5_b_re"][l][g].T
            s5b[r0:r0 + 16, 1, uc, g2 * 64:(g2 + 1) * 64] = p["s5_b_im"][l][g].T
            s5b[q * 32:q * 32 + 32, 2, uc, g2 * 64:(g2 + 1) * 64] = are[g][None, :]
            s5b[q * 32:q * 32 + 32, 3, uc, g2 * 64:(g2 + 1) * 64] = aim[g][None, :]
            s5b[q * 32:q * 32 + 32, 4, uc, g2 * 64:(g2 + 1) * 64] = ldt[g]
    s5c = np.zeros((128, 2, 16, 128), np.float32)
    for sc in range(16):
        for g2 in range(2):
            g = 2 * sc + g2
            c0 = ((sc % 4) * 2 + g2) * 16
            s5c[g2 * 64:(g2 + 1) * 64, 0, sc, c0:c0 + 16] = p["s5_c_re"][l][g].T
            s5c[g2 * 64:(g2 + 1) * 64, 1, sc, c0:c0 + 16] = p["s5_c_im"][l][g].T
    return pk, rgw.reshape(128, -1), s5b.reshape(128, -1), s5c.reshape(128, -1)


def make_in_maps(inp):
    p = {k: np.asarray(v, dtype=np.float32) for k, v in inp.items()}
    packs = [_pack_layer(l, p) for l in range(NL)]
    shared = {
        "w_in": p["w_in"], "rg_w_out": p["rg_w_out"], "s5_glu_w": p["s5_glu_w"], "s5_w_out": p["s5_w_out"],
        "gdn_w_out": p["gdn_w_out"], "w_out": p["w_out"], "ffn_w_gate": p["ffn_w_gate"], "ffn_w_up": p["ffn_w_up"],
        "ffn_w_down": p["ffn_w_down"],
        "pk": np.stack([x[0] for x in packs]), "rgw": np.stack([x[1] for x in packs]),
        "s5b": np.stack([x[2] for x in packs]), "s5c": np.stack([x[3] for x in packs]),
        "cst": _consts(),
    }
    maps = []
    for c in range(8):
        bs = slice(NB * c, NB * c + NB)
        xT = np.concatenate([p["x_prompt"][c].T, p["x_sample"][bs].reshape(NS_TOK, D).T], axis=1)
        s5 = lambda a: a[:, bs].reshape(NL, NB, 16, 2, 64).transpose(0, 3, 4, 2, 1).reshape(NL, 128, 16, NB)
        m = dict(shared)
        m.update({
            "xT": np.ascontiguousarray(xT),
            "st_rgconv": np.ascontiguousarray(p["state_rg_conv"][:, bs].transpose(0, 3, 1, 2)),
            "st_rgh": np.ascontiguousarray(p["state_rg_h"][:, bs].transpose(0, 2, 1)),
            "st_s5re": np.ascontiguousarray(s5(p["state_s5_re"])),
            "st_s5im": np.ascontiguousarray(s5(p["state_s5_im"])),
            "st_gdconv": np.ascontiguousarray(p["state_gdn_conv"][:, bs].transpose(0, 3, 1, 2)),
            "st_gds": np.ascontiguousarray(p["state_gdn_s"][:, bs]),
        })
        maps.append(m)
    return maps


def assemble(results):
    f = np.float32
    y_prompt = np.zeros((8, NP_TOK, D), f)
    y_sample = np.zeros((128, 8, D), f)
    p_rg_conv = np.zeros((NL, 8, 3, 512), f)
    p_rg_h = np.zeros((NL, 8, 512), f)
    p_s5_re = np.zeros((NL, 8, 32, 64), f)
    p_s5_im = np.zeros((NL, 8, 32, 64), f)
    p_gd_conv = np.zeros((NL, 8, 3, 1536), f)
    p_gd_s = np.zeros((NL, 8, 4, 128, 128), f)
    s_rg_conv = np.zeros((NL, 128, 3, 512), f)
    s_rg_h = np.zeros((NL, 128, 512), f)
    s_s5_re = np.zeros((NL, 128, 32, 64), f)
    s_s5_im = np.zeros((NL, 128, 32, 64), f)
    s_gd_conv = np.zeros((NL, 128, 3, 1536), f)
    s_gd_s = np.zeros((NL, 128, 4, 128, 128), f)
    for c, r in enumerate(results):
        bs = slice(NB * c, NB * c + NB)
        yT = r["yT"]
        y_prompt[c] = yT[:, :NP_TOK].T
        y_sample[bs] = yT[:, NP_TOK:].T.reshape(NB, 8, D)
        p_rg_conv[:, c] = r["o_p_rgconv"].transpose(0, 2, 1)
        p_rg_h[:, c] = r["o_p_rgh"][:, :, 0]
        un5 = lambda a: a.reshape(NL, 2, 64, 16).transpose(0, 3, 1, 2).reshape(NL, 32, 64)
        p_s5_re[:, c] = un5(r["o_p_s5re"])
        p_s5_im[:, c] = un5(r["o_p_s5im"])
        p_gd_conv[:, c] = r["o_p_gdconv"].transpose(0, 2, 1)
        p_gd_s[:, c] = r["o_p_gds"]
        s_rg_conv[:, bs] = r["o_s_rgconv"].transpose(0, 2, 3, 1)
        s_rg_h[:, bs] = r["o_s_rgh"].transpose(0, 2, 1)
        un5s = lambda a: a.reshape(NL, 2, 64, 16, NB).transpose(0, 4, 3, 1, 2).reshape(NL, NB, 32, 64)
        s_s5_re[:, bs] = un5s(r["o_s_s5re"])
        s_s5_im[:, bs] = un5s(r["o_s_s5im"])
        s_gd_conv[:, bs] = r["o_s_gdconv"].transpose(0, 2, 3, 1)
        s_gd_s[:, bs] = r["o_s_gds"]
    return (y_prompt, y_sample, p_rg_conv, p_rg_h, p_s5_re, p_s5_im, p_gd_conv, p_gd_s,
            s_rg_conv, s_rg_h, s_s5_re, s_s5_im, s_gd_conv, s_gd_s)


_CACHE = {}


def kernel(**inputs):
    if "nc" not in _CACHE:
        _CACHE["nc"] = build_program()[0]
    nc = _CACHE["nc"]
    maps = make_in_maps(inputs)
    res = run_bass_kernel_spmd(nc, maps, core_ids=list(range(8)))
    return assemble(res.results)
```

```python
import os
import numpy as np
from contextlib import ExitStack
import concourse.bass as bass
import concourse.mybir as mybir
from concourse.bass_utils import run_bass_kernel_spmd

F32 = mybir.dt.float32
BF16 = mybir.dt.bfloat16
F32R = mybir.dt.float32r
AF = mybir.ActivationFunctionType
ALU = mybir.AluOpType

NL = 2
D = 1024
KD = 8
NIN = 6664
DFF = 2816
KF = 22
NP_TOK = 2048
NS_TOK = 128
NT_TOK = NP_TOK + NS_TOK
NB = 16
EPS = 1e-6
GROUPS = [(0, 512, 1, 512), (512, 512, 1, 512), (1024, 512, 1, 512), (1536, 512, 1, 512),
          (2048, 128, 16, 8)]
C_RGX, C_RGG, C_S5U, C_QKV, C_Z, C_AB, C_GATE = 0, 512, 1024, 1536, 3072, 3584, 3592
WD_P = 32

PK = {}
_o = 0
for _n, _w in [("g_pre", 8), ("g_post", 8), ("g_pref", 8), ("g_postf", 8), ("rg_cw", 16), ("rg_cb", 4),
               ("rg_ba", 4), ("rg_bx", 4), ("rg_lam", 4), ("s5_d", 4), ("s5_glub", 4), ("gd_cw", 48),
               ("gd_nw", 1), ("s5_are", 16), ("s5_aim", 16), ("s5_ldt", 16), ("gd_alog", 4), ("gd_dtb", 4),
               ("w_ab", 64)]:
    PK[_n] = (_o, _w)
    _o += _w
NPK = _o


class Tile:
    __slots__ = ("h", "w", "r", "name", "psum")

    def __init__(self, h, name=""):
        self.h = h
        self.w = None
        self.r = {}
        self.name = name
        self.psum = False

    def __getitem__(self, idx):
        return Vw(self, self.h[idx])


class Vw:
    __slots__ = ("t", "a")

    def __init__(self, t, a):
        self.t = t
        self.a = a

    def __getitem__(self, idx):
        return Vw(self.t, self.a[idx])

    def re(self, pat, **kw):
        return Vw(self.t, self.a.rearrange(pat, **kw))

    def bc(self, pos, n):
        ap = [list(x) for x in self.a.ap]
        ap.insert(1 + pos, [0, n])
        return Vw(self.t, bass.AP(self.a.tensor, self.a.offset, ap))


def dram(ap):
    return Vw(None, ap)


class Sched:
    ENGS = ("tensor", "vector", "scalar", "gpsimd", "sync")

    def __init__(self, nc, es, n_dma_slots=8):
        self.nc = nc
        self.es = es
        self.prog = {e: [] for e in self.ENGS}
        self.sems = {}
        self.count = {}
        self.seen = {e: {} for e in self.ENGS}
        for e in self.ENGS:
            self.sems[e] = es.enter_context(nc.semaphore("s_" + e))
            self.count[e] = 0
        self.dma_slots = {}
        self.dma_next = {}
        for q in ("sync", "scalar", "gpsimd"):
            sl = []
            for i in range(n_dma_slots):
                key = "d_%s_%d" % (q, i)
                self.sems[key] = es.enter_context(nc.semaphore(key))
                self.count[key] = 0
                sl.append(key)
            self.dma_slots[q] = sl
            self.dma_next[q] = 0
        self.n_tiles = 0
        self.pools = {}
        self.out_deps = []
        self.sb_bytes = 0

    def sb(self, shape, dtype=F32, name=None):
        self.n_tiles += 1
        name = (name or "t") + "_%d" % self.n_tiles
        h = self.es.enter_context(self.nc.sbuf_tensor(name, list(shape), dtype))
        n = 1
        for s in shape[1:]:
            n *= s
        self.sb_bytes += n * (2 if dtype == BF16 else 4)
        return Tile(h, name)

    def ps(self, shape, dtype=F32, name=None):
        self.n_tiles += 1
        name = (name or "p") + "_%d" % self.n_tiles
        h = self.es.enter_context(self.nc.psum_tensor(name, list(shape), dtype))
        t = Tile(h, name)
        t.psum = True
        return t

    def alloc(self, kind):
        shape, dtype = {"F": ([128, 516], F32), "H": ([128, 512], BF16), "R": ([128, 2048], F32),
                        "C": ([128, 64], F32), "Q": ([128, 512], F32R)}[kind]
        fl = self.pools.setdefault(kind, [])
        if fl:
            return fl.pop(0)
        t = self.sb(shape, dtype, name=kind)
        t.name = kind
        self.pool_n = getattr(self, "pool_n", {})
        self.pool_n[kind] = self.pool_n.get(kind, 0) + 1
        return t

    def free(self, *tiles):
        for t in tiles:
            self.pools[t.name].append(t)

    def _deps(self, eng, reads, writes):
        deps = {}

        def add(k, v, skip_same):
            if k == eng and skip_same:
                return
            if deps.get(k, 0) < v:
                deps[k] = v

        for t in reads:
            if t is not None and t.w is not None:
                add(t.w[0], t.w[1], False)
        pe = (eng == "tensor")
        for t in writes:
            if t is None:
                continue
            if t.w is not None:
                add(t.w[0], t.w[1], pe)
            for k, v in t.r.items():
                add(k, v, pe)
        out = []
        seen = self.seen[eng]
        for k, v in deps.items():
            if seen.get(k, 0) < v:
                seen[k] = v
                out.append((k, v))
        return out

    def _mark(self, me, reads, writes):
        for t in writes:
            if t is not None:
                t.w = me
                t.r = {}
        for t in reads:
            if t is not None:
                if t.r.get(me[0], 0) < me[1]:
                    t.r[me[0]] = me[1]

    def op(self, eng, fn, reads=(), writes=()):
        pr = [t for t in reads if t is not None and t.psum]
        if pr:
            reads = [t for t in reads if t is None or not t.psum]
            writes = list(writes) + pr
        waits = self._deps(eng, reads, writes)
        self.count[eng] += 1
        me = (eng, self.count[eng])
        self.prog[eng].append((waits, fn, (eng, 1)))
        self._mark(me, reads, writes)

    def dma(self, q, fn, reads=(), writes=()):
        slots = self.dma_slots[q]
        key = slots[self.dma_next[q] % len(slots)]
        self.dma_next[q] += 1
        waits = self._deps(q, reads, writes)
        prev = self.count[key]
        if prev > 0 and self.seen[q].get(key, 0) < prev:
            self.seen[q][key] = prev
            waits.append((key, prev))
        self.count[key] += 16
        me = (key, self.count[key])
        self.prog[q].append((waits, fn, (key, 16)))
        self._mark(me, reads, writes)
        return me

    def wait_all(self, eng, deps):
        waits = []
        for k, v in deps:
            if self.seen[eng].get(k, 0) < v:
                self.seen[eng][k] = v
                waits.append((k, v))
        if waits:
            self.prog[eng].append((waits, None, None))

    def emit(self):
        nc = self.nc
        sems = self.sems
        with nc.Block() as block:
            for e in self.ENGS:
                prog = self.prog[e]
                if not prog:
                    continue

                def body(engine, prog=prog):
                    for waits, fn, inc in prog:
                        for k, v in waits:
                            engine.wait_ge(sems[k], v)
                        if fn is not None:
                            ins = fn(engine)
                            ins.then_inc(sems[inc[0]], inc[1])

                getattr(block, e)(body)


class Ops:
    def __init__(self, S):
        self.S = S

    @staticmethod
    def _ts(*vs):
        return [v.t for v in vs if isinstance(v, Vw)]

    @staticmethod
    def _a(v):
        return v.a if isinstance(v, Vw) else v

    def act(self, out, in_, func, scale=1.0, bias=None, eng="scalar"):
        kw = {"scale": self._a(scale)}
        if bias is not None:
            kw["bias"] = self._a(bias)
        self.S.op(eng, lambda e: e.activation(out=out.a, in_=in_.a, func=func, **kw),
                  self._ts(in_, scale, bias), self._ts(out))

    def tt(self, out, a, b, op, eng="vector"):
        self.S.op(eng, lambda e: e.tensor_tensor(out=out.a, in0=a.a, in1=b.a, op=op), self._ts(a, b), self._ts(out))

    def ts(self, out, a, s1, op0, s2=None, op1=None, eng="vector"):
        if op1 is None:
            fn = lambda e: e.tensor_scalar(out=out.a, in0=a.a, scalar1=self._a(s1), scalar2=None, op0=op0)
        else:
            fn = lambda e: e.tensor_scalar(out=out.a, in0=a.a, scalar1=self._a(s1), scalar2=self._a(s2), op0=op0, op1=op1)
        self.S.op(eng, fn, self._ts(a, s1, s2), self._ts(out))

    def stt(self, out, in0, scalar, in1, op0, op1):
        self.S.op("vector", lambda e: e.scalar_tensor_tensor(out=out.a, in0=in0.a, scalar=self._a(scalar), in1=in1.a,
                                                             op0=op0, op1=op1),
                  self._ts(in0, scalar, in1), self._ts(out))

    def scan(self, out, d0, d1, init, op0=ALU.mult, op1=ALU.add):
        self.S.op("vector", lambda e: e.tensor_tensor_scan(out=out.a, data0=d0.a, data1=d1.a, initial=self._a(init),
                                                           op0=op0, op1=op1),
                  self._ts(d0, d1, init), self._ts(out))

    def cp(self, out, in_, eng="scalar"):
        if eng == "scalar":
            fn = lambda e: e.copy(out=out.a, in_=in_.a)
        else:
            fn = lambda e: e.tensor_copy(out=out.a, in_=in_.a)
        self.S.op(eng, fn, self._ts(in_), self._ts(out))

    def memset(self, out, val, eng="vector"):
        self.S.op(eng, lambda e: e.memset(out.a, val), [], self._ts(out))

    def mm(self, out, lhsT, rhs, start=True, stop=True):
        self.S.op("tensor", lambda e: e.matmul(out.a, lhsT.a, rhs.a, start=start, stop=stop),
                  self._ts(lhsT, rhs), self._ts(out))

    def tr(self, out, in_, ident):
        self.S.op("tensor", lambda e: e.transpose(out.a, in_.a, ident.a), self._ts(in_, ident), self._ts(out))

    def dma(self, out, in_, q="sync", is_out=False, ncd=False):
        if ncd:
            fn = lambda e: e.dma_start(out=out.a, in_=in_.a, allow_slow_non_contiguous=True)
        else:
            fn = lambda e: e.dma_start(out=out.a, in_=in_.a)
        d = self.S.dma(q, fn, self._ts(in_), self._ts(out))
        if is_out:
            self.S.out_deps.append(d)
        return d


class Rot:
    def __init__(self, tiles):
        self.tiles = tiles
        self.i = 0

    def next(self):
        t = self.tiles[self.i % len(self.tiles)]
        self.i += 1
        return t


class _Stop(Exception):
    pass


def build_program(dbg=None, n_groups=5, n_layers=NL, stop=None, glist=None):
    nc = bass.Bass("TRN2", target_bir_lowering=False)
    dbg = dbg or {}

    n_groups_total = len(glist) if glist is not None else n_groups

    def ckpt(name):
        if stop == name:
            raise _Stop()

    def din(name, shape):
        return nc.dram_tensor(name, list(shape), F32, kind="ExternalInput").ap()

    def dout(name, shape):
        return nc.dram_tensor(name, list(shape), F32, kind="ExternalOutput").ap()

    xT_in = din("xT", [D, NT_TOK])
    st_rgconv = din("st_rgconv", [NL, 512, NB, 3])
    st_rgh = din("st_rgh", [NL, 512, NB])
    st_s5re = din("st_s5re", [NL, 128, 16, NB])
    st_s5im = din("st_s5im", [NL, 128, 16, NB])
    st_gdconv = din("st_gdconv", [NL, 1536, NB, 3])
    st_gds = din("st_gds", [NL, NB, 4, 128, 128])
    w_in = din("w_in", [NL, D, NIN])
    rg_w_out = din("rg_w_out", [NL, 512, D])
    s5_glu_w = din("s5_glu_w", [NL, 512, 512])
    s5_w_out = din("s5_w_out", [NL, 512, D])
    gdn_w_out = din("gdn_w_out", [NL, 512, D])
    w_out = din("w_out", [NL, D, D])
    ffn_w_gate = din("ffn_w_gate", [NL, D, DFF])
    ffn_w_up = din("ffn_w_up", [NL, D, DFF])
    ffn_w_down = din("ffn_w_down", [NL, DFF, D])
    pk_in = din("pk", [NL, 128, NPK])
    rgw_in = din("rgw", [NL, 128, 2 * 4 * 128])
    s5b_in = din("s5b", [NL, 128, 5 * 512])
    s5c_in = din("s5c", [NL, 128, 2 * 2048])
    cst_in = din("cst", [128, 128 * 7 + 20])

    def dscr(name, shape):
        return nc.dram_tensor(name, list(shape), BF16, kind="Internal").ap()

    WT = {"w_in": w_in, "rg_w_out": rg_w_out, "s5_glu_w": s5_glu_w, "s5_w_out": s5_w_out, "gdn_w_out": gdn_w_out,
          "w_out": w_out, "ffn_w_gate": ffn_w_gate, "ffn_w_up": ffn_w_up, "ffn_w_down": ffn_w_down}
    NWBLK = 448
    wscr = dscr("wscr", [NWBLK, 128, 2048])

    yT_out = dout("yT", [D, NT_TOK])
    o_p_rgconv = dout("o_p_rgconv", [NL, 512, 3])
    o_p_rgh = dout("o_p_rgh", [NL, 512, 1])
    o_p_s5re = dout("o_p_s5re", [NL, 128, 16])
    o_p_s5im = dout("o_p_s5im", [NL, 128, 16])
    o_p_gdconv = dout("o_p_gdconv", [NL, 1536, 3])
    o_p_gds = dout("o_p_gds", [NL, 4, 128, 128])
    o_s_rgconv = dout("o_s_rgconv", [NL, 512, NB, 3])
    o_s_rgh = dout("o_s_rgh", [NL, 512, NB])
    o_s_s5re = dout("o_s_s5re", [NL, 128, 16, NB])
    o_s_s5im = dout("o_s_s5im", [NL, 128, 16, NB])
    o_s_gdconv = dout("o_s_gdconv", [NL, 1536, NB, 3])
    o_s_gds = dout("o_s_gds", [NL, NB, 4, 128, 128])
    dbg_out = {k: dout("dbg_" + k, shp) for k, shp in dbg.items()}

    with ExitStack() as es:
        S = Sched(nc, es)
        O = Ops(S)
        act, tt, ts, stt, scan, cp, mm, tr, memset, dma = O.act, O.tt, O.ts, O.stt, O.scan, O.cp, O.mm, O.tr, O.memset, O.dma
        F = lambda: S.alloc("F")
        H = lambda: S.alloc("H")
        R = lambda: S.alloc("R")
        C = lambda: S.alloc("C")
        free = S.free
        IOQ = "scalar"
        INV_F32R = False
        rr = lambda v: v
        f32v = (lambda v: Vw(v.t, v.a.bitcast(F32))) if INV_F32R else (lambda v: v)
        QF = (lambda: S.alloc("Q")) if INV_F32R else (lambda: S.alloc("F"))
        S5_POST = "vector"

        def dump(name, vw):
            if name in dbg_out:
                if vw.a.dtype != F32:
                    f_ = F()
                    n_ = vw.a.shape[1]
                    cp(f_[:, 0:n_], vw, eng="vector")
                    dma(dram(dbg_out[name]), f_[:, 0:n_], q="sync", is_out=True, ncd=True)
                    free(f_)
                else:
                    dma(dram(dbg_out[name]), vw, q="sync", is_out=True, ncd=True)

        pA_t = [S.ps([128, 512], F32, "pA") for _ in range(3)]
        pA = Rot(pA_t)
        pB_t = [S.ps([128, 512], F32, "pB") for _ in range(4)]
        pB = Rot(pB_t)
        pT = S.ps([128, 512], F32, "pT")

        cst = S.sb([128, 128 * 7 + 20], F32, "cst")
        dma(cst[:], dram(cst_in), q=IOQ)
        ident_f = cst[:, 0:128]
        maskI_p, maskS_p, maskA_p = cst[:, 128:256], cst[:, 256:384], cst[:, 384:512]
        maskI_s, maskS_s, maskA_s = cst[:, 512:640], cst[:, 640:768], cst[:, 768:896]
        bmask = cst[:, 896:912]
        bandmask = cst[:, 912:916]
        ident_b = S.sb([128, 128], BF16, "identb")
        cp(ident_b[:], ident_f, eng="vector")
        ones_b = S.sb([128, 128], BF16, "onesb")
        memset(ones_b[:], 1.0)
        cvec = S.sb([128, 4], F32, "cvec")
        memset(cvec[:, 0:1], EPS)
        memset(cvec[:, 1:2], 1.0)
        memset(cvec[:, 2:3], float(np.pi / 2))
        memset(cvec[:, 3:4], 0.0)
        eps_c, one_c, hpi_c = cvec[:, 0:1], cvec[:, 1:2], cvec[:, 2:3]
        sel = S.sb([8, 8 * 128], F32, "sel")
        cp(sel[:].re("p (r m) -> p r m", r=8), cst[0:8, 0:8].bc(1, 128), eng="vector")
        resetP = S.sb([128, 512], F32, "resetP")
        memset(resetP[:], 1.0)
        memset(resetP[:].re("p (c t) -> p c t", t=128)[:, :, 0:1], 0.0)
        resetS = S.sb([128, 128], F32, "resetS")
        memset(resetS[:], 1.0)
        memset(resetS[:].re("p (c t) -> p c t", t=8)[:, :, 0:1], 0.0)

        pk, rgw_b, BTre, BTim, CTre, CTimn, cosT, sinT, rhoT, lam_re, lam_im, W_re, W_im = ([None] * NL for _ in range(13))
        c1, nA, wab_b, rho_c, BT3re, BT3im, diagD = ([None] * NL for _ in range(7))
        hist_rg, hist_gd, hlast_rg, rin_re, rin_im, Sp, Sp_b = ([None] * NL for _ in range(7))

        def pkv(l, name, j=None, w=1):
            o, n = PK[name]
            if j is None:
                return pk[l][:, o:o + n]
            return pk[l][:, o + j:o + j + w]

        def csq(c, s, n):
            t1, t2 = F(), F()
            tt(t1[:, 0:n], c, c, ALU.mult)
            tt(t2[:, 0:n], s, s, ALU.mult)
            tt(s, c, s, ALU.mult)
            tt(c, t1[:, 0:n], t2[:, 0:n], ALU.subtract)
            ts(s, s, 2.0, ALU.mult)
            free(t1, t2)

        def cmul(o_re, o_im, a_re, a_im, b_re, b_im, n):
            t1, t2 = F(), F()
            sh = lambda t: Vw(t.t, t.a) if False else t
            v1 = t1[:, 0:n]
            v2 = t2[:, 0:n]
            if len(o_re.a.shape) == 3:
                k = o_re.a.shape[1]
                v1 = v1.re("p (a b) -> p a b", a=k)
                v2 = v2.re("p (a b) -> p a b", a=k)
            tt(v1, a_re, b_re, ALU.mult)
            tt(v2, a_im, b_im, ALU.mult)
            tt(o_re, v1, v2, ALU.subtract)
            tt(v1, a_re, b_im, ALU.mult)
            tt(v2, a_im, b_re, ALU.mult)
            tt(o_im, v1, v2, ALU.add)
            free(t1, t2)

        for l in range(n_layers):
            pk[l] = S.sb([128, NPK], F32, "pk")
            dma(pk[l][:], dram(pk_in[l]), q=IOQ)
            stg = R()
            dma(stg[:, 0:1024], dram(rgw_in[l]), q=IOQ)
            rgw_b[l] = S.sb([128, 1024], BF16, "rgwb")
            cp(rgw_b[l][:], stg[:, 0:1024], eng="vector")
            free(stg)
            c1[l] = S.sb([128, 4], F32, "c1")
            act(c1[l][:], pkv(l, "rg_lam"), AF.Exp, scale=-1.0)
            act(c1[l][:], c1[l][:], AF.Ln, bias=one_c)
            ts(c1[l][:], c1[l][:], -8.0, ALU.mult)
            nA[l] = S.sb([128, 4], F32, "nA")
            act(nA[l][:], pkv(l, "gd_alog"), AF.Exp)
            ts(nA[l][:], nA[l][:], -1.0, ALU.mult)
            diagD[l] = S.sb([128, 512], BF16, "diagD")
            for oc_ in range(4):
                ts(diagD[l][:, 128 * oc_:128 * oc_ + 128], ident_f, pkv(l, "s5_d", oc_), ALU.mult)
            wab_b[l] = S.sb([128, 64], BF16, "wab")
            cp(wab_b[l][:], pkv(l, "w_ab"), eng="vector")

            sv = S.sb([128, 16 * 12], F32, "s5v")
            col = lambda i: sv[:, 16 * i:16 * (i + 1)]
            dt_, ar_, th_, rho_, c_, s_ = col(0), col(1), col(2), col(3), col(4), col(5)
            lam_re[l], lam_im[l], W_re[l], W_im[l] = col(6), col(7), col(8), col(9)
            e_re, e_im = col(10), col(11)
            rho_c[l] = rho_
            act(dt_, pkv(l, "s5_ldt"), AF.Exp)
            tt(ar_, pkv(l, "s5_are"), dt_, ALU.mult)
            tt(th_, pkv(l, "s5_aim"), dt_, ALU.mult)
            act(rho_, ar_, AF.Exp)
            act(s_, th_, AF.Sin, scale=1.0 / 16)
            act(c_, th_, AF.Sin, scale=1.0 / 16, bias=hpi_c)
            for _ in range(4):
                csq(c_, s_, 16)
            tt(lam_re[l], rho_, c_, ALU.mult)
            tt(lam_im[l], rho_, s_, ALU.mult)
            cosT[l] = S.sb([128, 16 * WD_P], F32, "cosT")
            sinT[l] = S.sb([128, 16 * WD_P], F32, "sinT")
            c3 = cosT[l][:].re("p (s t) -> p s t", s=16)
            s3 = sinT[l][:].re("p (s t) -> p s t", s=16)
            memset(c3[:, :, 0:1], 1.0)
            memset(s3[:, :, 0:1], 0.0)
            cp(c3[:, :, 1:2], c_.re("p (s o) -> p s o", o=1), eng="vector")
            cp(s3[:, :, 1:2], s_.re("p (s o) -> p s o", o=1), eng="vector")
            cp(e_re, c_, eng="vector")
            cp(e_im, s_, eng="vector")
            m = 1
            while m < WD_P:
                csq(e_re, e_im, 16)
                m *= 2
                if m < WD_P:
                    cmul(c3[:, :, m:2 * m], s3[:, :, m:2 * m], c3[:, :, 0:m], s3[:, :, 0:m],
                         e_re.bc(1, m), e_im.bc(1, m), 16 * m)
            tt(W_re[l], rho_, e_re, ALU.mult)
            tt(W_im[l], rho_, e_im, ALU.mult)
            rhoT[l] = S.sb([128, 16 * WD_P], F32, "rhoT")
            r3 = rhoT[l][:].re("p (s t) -> p s t", s=16)
            cp(r3, rho_.bc(1, WD_P), eng="vector")
            memset(r3[:, :, 0:1], 0.0)

            bs = [R() for _ in range(3)]
            for i in range(3):
                dma(bs[i][:, 0:1024] if i < 2 else bs[i][:, 0:512], dram(s5b_in[l][:, 1024 * i:1024 * i + (1024 if i < 2 else 512)]), q=IOQ)
            Bre_z, Bim_z = bs[0][:, 0:512], bs[0][:, 512:1024]
            are_b, aim_b = bs[1][:, 0:512], bs[1][:, 512:1024]
            ldt_b = bs[2][:, 0:512]
            wk = [F() for _ in range(8)]
            dtb, arb, thb, rhob, cb, sb_, x_, y_ = [w[:, 0:512] for w in wk]
            act(dtb, ldt_b, AF.Exp)
            tt(arb, are_b, dtb, ALU.mult)
            tt(thb, aim_b, dtb, ALU.mult)
            act(rhob, arb, AF.Exp)
            act(sb_, thb, AF.Sin, scale=1.0 / 16)
            act(cb, thb, AF.Sin, scale=1.0 / 16, bias=hpi_c)
            for _ in range(4):
                csq(cb, sb_, 512)
            tt(x_, rhob, cb, ALU.mult)
            ts(x_, x_, -1.0, ALU.add)
            tt(y_, rhob, sb_, ALU.mult)
            den, sr, si = dtb, arb, thb
            tt(den, are_b, are_b, ALU.mult)
            tt(cb, aim_b, aim_b, ALU.mult)
            tt(den, den, cb, ALU.add)
            S.op("vector", lambda e, o=den.a: e.reciprocal(out=o, in_=o), [den.t], [den.t])
            tt(sr, x_, are_b, ALU.mult)
            tt(cb, y_, aim_b, ALU.mult)
            tt(sr, sr, cb, ALU.add)
            tt(sr, sr, den, ALU.mult)
            tt(si, y_, are_b, ALU.mult)
            tt(cb, x_, aim_b, ALU.mult)
            tt(si, si, cb, ALU.subtract)
            tt(si, si, den, ALU.mult)
            BTre[l] = S.sb([128, 512], BF16, "BTre")
            BTim[l] = S.sb([128, 512], BF16, "BTim")
            tt(x_, sr, Bre_z, ALU.mult)
            tt(y_, si, Bim_z, ALU.mult)
            tt(BTre[l][:], x_, y_, ALU.subtract)
            tt(x_, sr, Bim_z, ALU.mult)
            tt(y_, si, Bre_z, ALU.mult)
            tt(BTim[l][:], x_, y_, ALU.add)
            free(*wk)
            free(*bs)
            CTre[l] = S.sb([128, 2048], BF16, "CTre")
            CTimn[l] = S.sb([128, 2048], BF16, "CTimn")
            stg = R()
            dma(stg[:], dram(s5c_in[l][:, 0:2048]), q=IOQ)
            cp(CTre[l][:], stg[:], eng="vector")
            free(stg)
            stg = R()
            dma(stg[:], dram(s5c_in[l][:, 2048:4096]), q=IOQ)
            ts(CTimn[l][:], stg[:], -1.0, ALU.mult)
            free(stg)

            hist_rg[l] = S.sb([128, 12], F32, "hist_rg")
            hist_gd[l] = S.sb([128, 36], F32, "hist_gd")
            hlast_rg[l] = S.sb([128, 4], F32, "hlast")
            rin_re[l] = S.sb([128, 16], F32, "rin_re")
            rin_im[l] = S.sb([128, 16], F32, "rin_im")
            Sp[l] = S.sb([128, 512], F32, "Sp")
            Sp_b[l] = S.sb([128, 512], BF16, "Spb")
            for t_ in (hist_rg[l], hist_gd[l], hlast_rg[l], rin_re[l], rin_im[l], Sp[l], Sp_b[l]):
                memset(t_[:], 0.0)

        WSZ = 2048
        WBF = Rot([S.sb([128, WSZ], BF16, "wbf") for _ in range(6)])
        cast_rr = [0]

        wblk = {}

        def wload(wname, l_, k0, kn, c0, w):
            Wv = WT[wname][l_].rearrange("(k p) m -> p k m", p=128)
            wb = WBF.next()
            wv_ = wb[:, 0:kn * w].re("p (k m) -> p k m", k=kn)
            key = (wname, l_, c0, k0)
            if key not in wblk:
                dma(wv_, dram(Wv[:, k0:k0 + kn, c0:c0 + w]), q="gpsimd")
                if n_groups_total > 1:
                    bt = Tile(None, "wblk")
                    bid = len(wblk)
                    assert bid < NWBLK
                    wblk[key] = (bt, bid)
                    dma(Vw(bt, wscr[bid, :, 0:kn * w]), wb[:, 0:kn * w], q="sync")
            else:
                bt, bid = wblk[key]
                dma(wb[:, 0:kn * w], Vw(bt, wscr[bid, :, 0:kn * w]), q="sync")
            return wv_

        def dense(W, KC, col0, ncols, rhs, N, consume):
            wname, l_ = W
            if KC * 128 > WSZ:
                npc = -(-KC * 128 // WSZ)
                kp = -(-KC // npc)
                for mi, c0 in enumerate(range(col0, col0 + ncols, 128)):
                    ps = pA.next()
                    for k0 in range(0, KC, kp):
                        kn = min(kp, KC - k0)
                        wv_ = wload(wname, l_, k0, kn, c0, 128)
                        for k in range(kn):
                            mm(ps[:, 0:N], wv_[:, k, :], rhs(k0 + k), start=(k0 + k == 0), stop=(k0 + k == KC - 1))
                    consume(mi, 128, ps)
                return
            bw = min(ncols, 512, max(128, (WSZ // KC) // 128 * 128))
            mi = 0
            for c0 in range(col0, col0 + ncols, bw):
                w = min(bw, col0 + ncols - c0)
                wv_ = wload(wname, l_, 0, KC, c0, w)
                for m0 in range(0, w, 128):
                    msz = min(128, w - m0)
                    ps = pA.next()
                    for k in range(KC):
                        mm(ps[0:msz, 0:N], wv_[:, k, m0:m0 + msz], rhs(k), start=(k == 0), stop=(k == KC - 1))
                    consume(mi, msz, ps)
                    mi += 1

        def rms_rstd(srcs, N, scale):
            ps = pB.next()
            n = len(srcs)
            for k, sv_ in enumerate(srcs):
                sq = H()
                act(sq[:, 0:N], sv_, AF.Square)
                mm(ps[:, 0:N], ones_b[:], sq[:, 0:N], start=(k == 0), stop=(k == n - 1))
                free(sq)
            r = F()
            act(r[:, 0:N], ps[:, 0:N], AF.Ln, scale=scale, bias=eps_c)
            act(r[:, 0:N], r[:, 0:N], AF.Exp, scale=-0.5)
            return r

        try:
          ckpt("prep")
          for gi, (tok0, N, B, T) in [(i, GROUPS[i]) for i in (glist if glist is not None else range(n_groups))]:
              prompt = (B == 1)
              last_p = (gi == 3)
              Wc = T + 3
              xT = [F() for _ in range(KD)]
              xin = xT_in.rearrange("(k p) n -> p k n", p=128)
              for k in range(KD):
                  dma(xT[k][:, 0:N], dram(xin[:, k, tok0:tok0 + N]), q=IOQ)

              for l in range(n_layers):
                  rstd = rms_rstd([xT[k][:, 0:N] for k in range(KD)], N, 1.0 / D)
                  hT = [H() for _ in range(KD)]
                  for k in range(KD):
                      stt(hT[k][:, 0:N], xT[k][:, 0:N], pkv(l, "g_pre", k), rstd[:, 0:N], ALU.mult, ALU.mult)
                  free(rstd)
                  rhs_h = lambda k: hT[k][:, 0:N]

                  ckpt("norm")
                  xcs = [None] * 4

                  if not prompt:
                      cvi_rg, cvo_rg = F(), F()
                      dma(cvi_rg[:, 0:192].re("p (c x) -> p c x", c=4), dram(st_rgconv[l].rearrange("(c p) b k -> p c (b k)", p=128)), q=IOQ)

                  def rg_x(mi, msz, ps):
                      xp = F()
                      x3 = xp[:, 0:B * Wc].re("p (b t) -> p b t", b=B)
                      if prompt:
                          cp(x3[:, 0, 0:3], hist_rg[l][:, 3 * mi:3 * mi + 3])
                      else:
                          cp(x3[:, :, 0:3], cvi_rg[:, 48 * mi:48 * mi + 48].re("p (b k) -> p b k", b=NB), eng="vector")
                      cp(x3[:, :, 3:Wc], ps[:, 0:N].re("p (b t) -> p b t", b=B))
                      if prompt:
                          cp(hist_rg[l][:, 3 * mi:3 * mi + 3], x3[:, 0, T:T + 3], eng="vector")
                          if last_p:
                              dma(dram(o_p_rgconv[l, 128 * mi:128 * mi + 128]), hist_rg[l][:, 3 * mi:3 * mi + 3], q=IOQ, is_out=True, ncd=True)
                      else:
                          cp(cvo_rg[:, 48 * mi:48 * mi + 48].re("p (b k) -> p b k", b=NB), x3[:, :, T:T + 3], eng="vector")
                          if mi == 3:
                              dma(dram(o_s_rgconv[l].rearrange("(c p) b k -> p c (b k)", p=128)), cvo_rg[:, 0:192].re("p (c x) -> p c x", c=4), q=IOQ, is_out=True)
                              free(cvi_rg, cvo_rg)
                      xc = F()
                      xc3 = xc[:, 0:N].re("p (b t) -> p b t", b=B)
                      o4 = PK["rg_cw"][0] + 4 * mi
                      ts(xc3, x3[:, :, 0:T], pk[l][:, o4:o4 + 1], ALU.mult, pkv(l, "rg_cb", mi), ALU.add)
                      for kk in range(1, 4):
                          stt(xc3, x3[:, :, kk:kk + T], pk[l][:, o4 + kk:o4 + kk + 1], xc3, ALU.mult, ALU.add)
                      free(xp)
                      xcs[mi] = xc

                  dense(("w_in", l), KD, C_RGX, 512, rhs_h, N, rg_x)
                  ggs = [None] * 4

                  def rg_g(mi, msz, ps):
                      gg = F()
                      act(gg[:, 0:N], ps[:, 0:N], AF.Gelu_apprx_tanh)
                      ggs[mi] = gg

                  dense(("w_in", l), KD, C_RGG, 512, rhs_h, N, rg_g)
                  y_rg = [None] * 4

                  def rg_gen(c):
                      xc = xcs[c]
                      xcb = H()
                      cp(xcb[:, 0:N], xc[:, 0:N])
                      yield
                      ps_r, ps_i = pB.next(), pB.next()
                      wv = rgw_b[l][:].re("p (a c m) -> p a c m", a=2, c=4)
                      mm(ps_r[:, 0:N], wv[:, 0, c, :], xcb[:, 0:N])
                      mm(ps_i[:, 0:N], wv[:, 1, c, :], xcb[:, 0:N])
                      free(xcb)
                      yield
                      r_, ig, a_ = F(), F(), F()
                      act(r_[:, 0:N], ps_r[:, 0:N], AF.Sigmoid, bias=pkv(l, "rg_ba", c))
                      act(ig[:, 0:N], ps_i[:, 0:N], AF.Sigmoid, bias=pkv(l, "rg_bx", c))
                      yield
                      act(a_[:, 0:N], r_[:, 0:N], AF.Exp, scale=c1[l][:, c:c + 1])
                      tt(ig[:, 0:N], ig[:, 0:N], xc[:, 0:N], ALU.mult)
                      yield
                      tt(r_[:, 0:N], a_[:, 0:N], a_[:, 0:N], ALU.mult)
                      yield
                      act(r_[:, 0:N], r_[:, 0:N], AF.Sqrt, scale=-1.0, bias=one_c)
                      yield
                      tt(ig[:, 0:N], ig[:, 0:N], r_[:, 0:N], ALU.mult)
                      hh = r_
                      if prompt:
                          scan(hh[:, 0:N], a_[:, 0:N], ig[:, 0:N], hlast_rg[l][:, c:c + 1])
                          cp(hlast_rg[l][:, c:c + 1], hh[:, N - 1:N], eng="vector")
                          if last_p:
                              dma(dram(o_p_rgh[l, 128 * c:128 * c + 128]), hlast_rg[l][:, c:c + 1], q=IOQ, is_out=True, ncd=True)
                      else:
                          h0 = C()
                          dma(h0[:, 0:NB], dram(st_rgh[l, 128 * c:128 * c + 128]), q=IOQ)
                          a3 = a_[:, 0:N].re("p (b t) -> p b t", b=B)
                          i3 = ig[:, 0:N].re("p (b t) -> p b t", b=B)
                          t0 = C()
                          tt(t0[:, 0:NB], a3[:, :, 0], h0[:, 0:NB], ALU.mult)
                          tt(i3[:, :, 0], i3[:, :, 0], t0[:, 0:NB], ALU.add)
                          memset(a3[:, :, 0:1], 0.0)
                          free(h0, t0)
                          xc2 = xc
                          scan(xc2[:, 0:N], a_[:, 0:N], ig[:, 0:N], 0.0)
                          hh = xc2
                          hl_ = C()
                          cp(hl_[:, 0:NB], hh[:, 0:N].re("p (b t) -> p b t", b=B)[:, :, T - 1], eng="vector")
                          dma(dram(o_s_rgh[l, 128 * c:128 * c + 128]), hl_[:, 0:NB], q=IOQ, is_out=True)
                          free(hl_)
                      yield
                      y = H()
                      tt(y[:, 0:N], hh[:, 0:N], ggs[c][:, 0:N], ALU.mult)
                      y_rg[c] = y
                      free(r_, ig, a_, xc, ggs[c])

                  for c0_ in (0, 2):
                      gens = [rg_gen(c0_), rg_gen(c0_ + 1)]
                      while gens:
                          for g_ in list(gens):
                              try:
                                  next(g_)
                              except StopIteration:
                                  gens.remove(g_)
                  if l == 0 and gi == 0:
                      for c in range(4):
                          dump("y_rg%d" % c, y_rg[c][:, 0:N])

                  ckpt("rg")
                  uf, ub = [None] * 4, [None] * 4

                  def s5_u(mi, msz, ps):
                      ub[mi] = H()
                      cp(ub[mi][:, 0:N], ps[:, 0:N])

                  dense(("w_in", l), KD, C_S5U, 512, rhs_h, N, s5_u)
                  nch = N // 128
                  RT = (lambda: R()) if prompt else (lambda: F())
                  rv = lambda t_: t_[:, 0:4 * N].re("p (h n) -> p h n", h=4)
                  psr = pB.next()
                  wab3 = wab_b[l][:].re("p (k j) -> p k j", k=8)
                  for k in range(KD):
                      mm(psr[0:8, 0:N], wab3[:, k, :], hT[k][:, 0:N], start=(k == 0), stop=(k == KD - 1))
                  R8 = F()
                  cp(R8[0:8, 0:N], psr[0:8, 0:N])
                  sel3 = sel[:].re("p (r m) -> p r m", r=8)
                  GC, BR, EG = RT(), RT(), RT()
                  for h in range(4):
                      pa = pB.next()
                      mm(pa[:, 0:N], sel3[:, h, :], R8[0:8, 0:N])
                      e1 = F()
                      act(e1[:, 0:N], pa[:, 0:N], AF.Exp, bias=pkv(l, "gd_dtb", h))
                      act(e1[:, 0:N], e1[:, 0:N], AF.Ln, bias=one_c)
                      act(rv(EG)[:, h, :], e1[:, 0:N], AF.Identity, scale=nA[l][:, h:h + 1])
                      free(e1)
                      pb_ = pB.next()
                      mm(pb_[:, 0:N], sel3[:, 4 + h, :], R8[0:8, 0:N])
                      act(rv(BR)[:, h, :], pb_[:, 0:N], AF.Sigmoid)
                  free(R8)
                  ckpt("s5_a")
                  yacc = [F() for _ in range(4)]
                  cols = [None] * nch

                  def col_gen(ci):
                      cs_ = slice(ci * 128, (ci + 1) * 128)
                      mI_, mA_ = (maskI_p, maskA_p) if prompt else (maskI_s, maskA_s)
                      pc = pB.next()
                      for k in range(KD):
                          mm(pc[:, 0:8], hT[k][:, cs_], wab3[:, k, :], start=(k == 0), stop=(k == KD - 1))
                      cc = C()
                      cols[ci] = cc
                      gcol, bcol, gccol, glast, bk, kd = (cc[:, 4 * i:4 * i + 4] for i in range(6))
                      yield
                      tt(gcol, pc[:, 0:4], pkv(l, "gd_dtb"), ALU.add)
                      act(gcol, gcol, AF.Exp)
                      act(gcol, gcol, AF.Ln, bias=one_c)
                      act(bcol, pc[:, 4:8], AF.Sigmoid)
                      yield
                      tt(gcol, gcol, nA[l][:], ALU.mult)
                      pg = pB.next()
                      mm(pg[:, 0:4], mI_, gcol)
                      mm(pg[:, 4:8], mA_, gcol)
                      yield
                      cp(gccol, pg[:, 0:4], eng="vector")
                      tt(glast, pg[:, 4:8], gccol, ALU.subtract)
                      act(kd, glast, AF.Exp)
                      act(bk, gccol, AF.Exp)
                      yield
                      tt(bk, bk, bcol, ALU.mult)

                  col_pending = [col_gen(ci) for ci in range(nch)]

                  def col_step():
                      while col_pending:
                          try:
                              next(col_pending[0])
                              return
                          except StopIteration:
                              col_pending.pop(0)
                  if prompt:
                      Wd, SB = WD_P, 16
                  else:
                      Wd, SB = 128, 4
                  nsub = N // Wd
                  nblk = 16 // SB
                  cT3 = cosT[l][:].re("p (s t) -> p s t", s=16)
                  sT3 = sinT[l][:].re("p (s t) -> p s t", s=16)
                  BTr = BTre[l][:].re("p (u m) -> p u m", u=4)
                  BTi = BTim[l][:].re("p (u m) -> p u m", u=4)
                  BT3r = BT3i = None
                  CTr = CTre[l][:].re("p (s m) -> p s m", s=16)
                  CTi = CTimn[l][:].re("p (s m) -> p s m", s=16)

                  ubq = [[None] * 4 for _ in range(4)]
                  for uc_ in range(4):
                      for q_ in range(4):
                          ubq[uc_][q_] = H()
                          ts(ubq[uc_][q_][:, 0:N], ub[uc_][:, 0:N], bandmask[:, q_:q_ + 1], ALU.mult)

                  def bmm(out, BT_, BT3_, sc, cols):
                      uc, q = sc // 4, sc % 4
                      mm(out, BT_[:, uc, :], ubq[uc][q][:, cols])

                  if not prompt:
                      s0r, s0i, rsr, rsi, fin_r, fin_i = F(), F(), F(), F(), F(), F()
                      dma(s0r[:, 0:256], dram(st_s5re[l].rearrange("p s b -> p (s b)")), q=IOQ)
                      dma(s0i[:, 0:256], dram(st_s5im[l].rearrange("p s b -> p (s b)")), q=IOQ)
                      v3 = lambda t_: t_[:, 0:256].re("p (s b) -> p s b", s=16)
                      cmul(v3(rsr), v3(rsi), v3(s0r), v3(s0i), lam_re[l].bc(1, NB), lam_im[l].bc(1, NB), 256)
                      free(s0r, s0i)
                      rhoS = R()
                      cp(rhoS[:].re("p (s t) -> p s t", s=16), rho_c[l].bc(1, 128), eng="vector")
                      memset(rhoS[:].re("p (c t) -> p c t", t=8)[:, :, 0:1], 0.0)
                  pre, pim = pA_t[0], pA_t[1]
                  s1, s2 = pA_t[2], pT
                  units = [(j, sb_i) for j in range(nsub) for sb_i in range(nblk)]

                  def emitB(u_):
                      j_, sb_ = units[u_]
                      for s_ in range(SB):
                          bmm(pre[:, s_ * Wd:(s_ + 1) * Wd], BTr, BT3r, sb_ * SB + s_, slice(j_ * Wd, j_ * Wd + Wd))
                      for s_ in range(SB):
                          bmm(pim[:, s_ * Wd:(s_ + 1) * Wd], BTi, BT3i, sb_ * SB + s_, slice(j_ * Wd, j_ * Wd + Wd))

                  emitB(0)
                  for ui, (j, sb_i) in enumerate(units):
                      if True:
                          sc0 = sb_i * SB
                          t0_ = j * Wd
                          if prompt:
                              dv = lambda t_: t_[:, 0:512].re("p (s t) -> p s t", s=SB)
                              cV, sV = cT3[:, sc0:sc0 + SB, 0:Wd], sT3[:, sc0:sc0 + SB, 0:Wd]
                              rV = rhoT[l][:, sc0 * Wd:(sc0 + SB) * Wd]
                          else:
                              dv = lambda t_: t_[:, 0:512].re("p (s b t) -> p s b t", s=SB, b=NB)
                              cV, sV = cT3[:, sc0:sc0 + SB, 0:8].bc(1, NB), sT3[:, sc0:sc0 + SB, 0:8].bc(1, NB)
                              rV = rhoS[:, sc0 * 128:(sc0 + SB) * 128]
                          ckpt("s5_b")
                          t1, t2, zr, zi = F(), F(), F(), F()
                          tt(dv(t1), dv(pre), cV, ALU.mult)
                          tt(dv(t2), dv(pim), sV, ALU.mult)
                          tt(zr[:, 0:512], t1[:, 0:512], t2[:, 0:512], ALU.add)
                          tt(dv(t1), dv(pim), cV, ALU.mult)
                          tt(dv(t2), dv(pre), sV, ALU.mult)
                          tt(zi[:, 0:512], t1[:, 0:512], t2[:, 0:512], ALU.subtract)
                          ckpt("s5_c")
                          if ui + 1 < len(units):
                              emitB(ui + 1)
                          col_step()
                          if prompt:
                              zr0 = dv(zr)[:, :, 0]
                              zi0 = dv(zi)[:, :, 0]
                              tt(zr0, zr0, rin_re[l][:, sc0:sc0 + SB], ALU.add)
                              tt(zi0, zi0, rin_im[l][:, sc0:sc0 + SB], ALU.add)
                          else:
                              zr0 = dv(zr)[:, :, :, 0]
                              zi0 = dv(zi)[:, :, :, 0]
                              tt(zr0, zr0, v3(rsr)[:, sc0:sc0 + SB, :], ALU.add)
                              tt(zi0, zi0, v3(rsi)[:, sc0:sc0 + SB, :], ALU.add)
                          ckpt("s5_d")
                          scan(s1[:, 0:512], rV, zr[:, 0:512], 0.0)
                          scan(s2[:, 0:512], rV, zi[:, 0:512], 0.0)
                          ckpt("s5_e")
                          ore, oim = H(), H()
                          p1, p2 = t1, t2
                          tt(dv(p1), dv(s1), cV, ALU.mult, eng=S5_POST)
                          tt(dv(p2), dv(s2), sV, ALU.mult, eng=S5_POST)
                          tt(ore[:], p1[:, 0:512], p2[:, 0:512], ALU.subtract, eng=S5_POST)
                          tt(dv(p1), dv(s1), sV, ALU.mult, eng=S5_POST)
                          tt(dv(p2), dv(s2), cV, ALU.mult, eng=S5_POST)
                          tt(oim[:], p1[:, 0:512], p2[:, 0:512], ALU.add, eng=S5_POST)
                          ckpt("s5_f")
                          if prompt:
                              slr, sli = dv(s1)[:, :, Wd - 1], dv(s2)[:, :, Wd - 1]
                              if last_p and j == nsub - 1:
                                  fr, fi = C(), C()
                                  cmul(fr[:, 0:16], fi[:, 0:16], slr, sli, cT3[:, :, Wd - 1], sT3[:, :, Wd - 1], 16)
                                  dma(dram(o_p_s5re[l]), fr[:, 0:16], q=IOQ, is_out=True)
                                  dma(dram(o_p_s5im[l]), fi[:, 0:16], q=IOQ, is_out=True)
                                  free(fr, fi)
                              cmul(rin_re[l][:], rin_im[l][:], slr, sli, W_re[l], W_im[l], 16)
                          else:
                              slr, sli = dv(s1)[:, :, :, T - 1], dv(s2)[:, :, :, T - 1]
                              cmul(v3(fin_r)[:, sc0:sc0 + SB, :], v3(fin_i)[:, sc0:sc0 + SB, :], slr, sli,
                                   cT3[:, sc0:sc0 + SB, T - 1].bc(1, NB), sT3[:, sc0:sc0 + SB, T - 1].bc(1, NB), SB * NB)
                          free(t1, t2, zr, zi)
                          ckpt("s5_g")
                          psY = pB.next()
                          for s_ in range(SB):
                              sc = sc0 + s_
                              oc = sc // 4
                              c0_ = oc * Wd if prompt else 0
                              mm(psY[:, c0_:c0_ + Wd], CTr[:, sc, :], ore[:, s_ * Wd:(s_ + 1) * Wd], start=(sc % 4 == 0), stop=False)
                              mm(psY[:, c0_:c0_ + Wd], CTi[:, sc, :], oim[:, s_ * Wd:(s_ + 1) * Wd], start=False, stop=False)
                              if sc % 4 == 3:
                                  mm(psY[:, c0_:c0_ + Wd], diagD[l][:, 128 * oc:128 * oc + 128], ub[oc][:, t0_:t0_ + Wd], start=False, stop=True)
                          for oc in sorted(set((sc0 + s_) // 4 for s_ in range(SB))):
                              c0_ = oc * Wd if prompt else 0
                              cp(yacc[oc][:, t0_:t0_ + Wd], psY[:, c0_:c0_ + Wd])
                          free(ore, oim)
                          ckpt("s5_h")
                  if not prompt:
                      dma(dram(o_s_s5re[l].rearrange("p s b -> p (s b)")), fin_r[:, 0:256], q=IOQ, is_out=True)
                      dma(dram(o_s_s5im[l].rearrange("p s b -> p (s b)")), fin_i[:, 0:256], q=IOQ, is_out=True)
                      free(rsr, rsi, fin_r, fin_i, rhoS)
                  for uc_ in range(4):
                      free(*ubq[uc_])
                  while col_pending:
                      col_step()
                  ygs, ygb = [None] * 4, [None] * 4
                  for oc in range(4):
                      yv = yacc[oc]
                      act(yv[:, 0:N], yv[:, 0:N], AF.Gelu_apprx_tanh)
                      ygb[oc] = H()
                      cp(ygb[oc][:, 0:N], yv[:, 0:N])
                      ygs[oc] = yv
                      free(ub[oc])
                  y_s5 = [None] * 4

                  def s5_glu(mi, msz, ps):
                      sg = F()
                      act(sg[:, 0:N], ps[:, 0:N], AF.Sigmoid, bias=pkv(l, "s5_glub", mi))
                      y_s5[mi] = H()
                      tt(y_s5[mi][:, 0:N], ygs[mi][:, 0:N], sg[:, 0:N], ALU.mult)
                      free(sg, ygs[mi])

                  dense(("s5_glu_w", l), 4, 0, 512, lambda k: ygb[k][:, 0:N], N, s5_glu)
                  free(*ygb)
                  if l == 0 and gi == 0:
                      for c in range(4):
                          dump("y_s5%d" % c, y_s5[c][:, 0:N])

                  ckpt("s5")
                  rmask = resetP[:, 0:N] if prompt else resetS[:, 0:N]
                  for h in range(4):
                      scan(rv(GC)[:, h, :], rmask, rv(EG)[:, h, :], 0.0)
                  act(EG[:, 0:4 * N], GC[:, 0:4 * N], AF.Exp)

                  if not prompt:
                      cvi_gd, cvo_gd = [F(), F()], [F(), F()]
                      for hf_ in range(2):
                          dma(cvi_gd[hf_][:, 0:288].re("p (c x) -> p c x", c=6),
                              dram(st_gdconv[l, 768 * hf_:768 * hf_ + 768].rearrange("(c p) b k -> p c (b k)", p=128)), q=IOQ)

                  def gd_x(mi, msz, ps):
                      xp = F()
                      x3 = xp[:, 0:B * Wc].re("p (b t) -> p b t", b=B)
                      if prompt:
                          cp(x3[:, 0, 0:3], hist_gd[l][:, 3 * mi:3 * mi + 3])
                      else:
                          cp(x3[:, :, 0:3], cvi_gd[mi // 6][:, 48 * (mi % 6):48 * (mi % 6) + 48].re("p (b k) -> p b k", b=NB), eng="vector")
                      cp(x3[:, :, 3:Wc], ps[:, 0:N].re("p (b t) -> p b t", b=B))
                      if prompt:
                          cp(hist_gd[l][:, 3 * mi:3 * mi + 3], x3[:, 0, T:T + 3], eng="vector")
                          if last_p:
                              dma(dram(o_p_gdconv[l, 128 * mi:128 * mi + 128]), hist_gd[l][:, 3 * mi:3 * mi + 3], q=IOQ, is_out=True, ncd=True)
                      else:
                          cp(cvo_gd[mi // 6][:, 48 * (mi % 6):48 * (mi % 6) + 48].re("p (b k) -> p b k", b=NB), x3[:, :, T:T + 3], eng="vector")
                          if mi % 6 == 5:
                              hf_ = mi // 6
                              dma(dram(o_s_gdconv[l, 768 * hf_:768 * hf_ + 768].rearrange("(c p) b k -> p c (b k)", p=128)),
                                  cvo_gd[hf_][:, 0:288].re("p (c x) -> p c x", c=6), q=IOQ, is_out=True)
                              free(cvi_gd[hf_], cvo_gd[hf_])
                      xc = F()
                      xc3 = xc[:, 0:N].re("p (b t) -> p b t", b=B)
                      o4 = PK["gd_cw"][0] + 4 * mi
                      ts(xc3, x3[:, :, 0:T], pk[l][:, o4:o4 + 1], ALU.mult)
                      for kk in range(1, 4):
                          stt(xc3, x3[:, :, kk:kk + T], pk[l][:, o4 + kk:o4 + kk + 1], xc3, ALU.mult, ALU.add)
                      free(xp)
                      act(xc[:, 0:N], xc[:, 0:N], AF.Silu)
                      prev = list(gd_pending)
                      del gd_pending[:]
                      if mi < 8:
                          sq = H()
                          act(sq[:, 0:N], xc[:, 0:N], AF.Square)

                          def fin(mi=mi, xc=xc, sq=sq):
                              pn = pB.next()
                              mm(pn[:, 0:N], ones_b[:], sq[:, 0:N])
                              free(sq)
                              rq = F()
                              act(rq[:, 0:N], pn[:, 0:N], AF.Ln, scale=1.0, bias=eps_c)
                              act(rq[:, 0:N], rq[:, 0:N], AF.Exp, scale=-0.5)
                              tt(xc[:, 0:N], xc[:, 0:N], rq[:, 0:N], ALU.mult)
                              free(rq)
                              if mi < 4:
                                  qb[mi], qgb[mi] = H(), H()
                                  ts(qb[mi][:, 0:N], xc[:, 0:N], 128.0 ** -0.5, ALU.mult)
                                  stt(qgb[mi][:, 0:N], xc[:, 0:N], 128.0 ** -0.5, rv(EG)[:, mi, :], ALU.mult, ALU.mult)
                                  free(xc)
                              else:
                                  kf[mi - 4] = xc
                                  kb16[mi - 4] = H()
                                  cp(kb16[mi - 4][:, 0:N], xc[:, 0:N])

                          gd_pending.append(fin)
                      else:
                          vf[mi - 8] = xc
                      for f_ in prev:
                          f_()

                  qb, qgb, kb16, kf, vf = [None] * 4, [None] * 4, [None] * 4, [None] * 4, [None] * 4
                  gd_pending = []
                  dense(("w_in", l), KD, C_QKV, 1536, rhs_h, N, gd_x)
                  for f_ in gd_pending:
                      f_()
                  EGl = C()
                  if prompt:
                      egl3 = EGl[:, 0:4 * nch].re("p (h c) -> p h c", h=4)
                      cp(egl3, rv(EG).re("p h (c t) -> p h c t", t=128)[:, :, :, 127], eng="vector")
                  else:
                      egl3 = EGl[:, 0:4 * NB].re("p (h b) -> p h b", h=4)
                      cp(egl3, rv(EG).re("p h (b t) -> p h b t", t=8)[:, :, :, 7], eng="vector")
                  free(EG)
                  OR = RT()
                  mI, mS, mA = (maskI_p, maskS_p, maskA_p) if prompt else (maskI_s, maskS_s, maskA_s)
                  nsteps = 6 if prompt else 2
                  v4 = lambda t_: t_[:, 0:512].re("p (h m) -> p h m", h=4)
                  decs = [None] * nch

                  def dec_gen(ci_):
                      cs_ = slice(ci_ * 128, (ci_ + 1) * 128)
                      gccol_ = cols[ci_][:, 8:12]
                      Dm_, E1_ = F(), F()
                      decs[ci_] = (Dm_, E1_)
                      tt(v4(Dm_), rv(GC)[:, :, cs_], gccol_.bc(1, 128), ALU.subtract)
                      ts(Dm_[:, 0:512], Dm_[:, 0:512], 0.0, ALU.min)
                      yield
                      act(Dm_[:, 0:512], Dm_[:, 0:512], AF.Exp)
                      yield
                      tt(v4(E1_), v4(Dm_), mS.bc(0, 4), ALU.mult)
                      yield
                      tt(v4(E1_), v4(E1_), rv(BR)[:, :, cs_], ALU.mult)
                      tt(v4(Dm_), v4(Dm_), mI.bc(0, 4), ALU.mult)

                  for _ in dec_gen(0):
                      pass
                  for ci in range(nch):
                      cs = slice(ci * 128, (ci + 1) * 128)
                      dec_next = dec_gen(ci + 1) if ci + 1 < nch else iter(())
                      cc = cols[ci]
                      gcol, bcol, gccol, glast, bk, kd = (cc[:, 4 * i:4 * i + 4] for i in range(6))
                      pk_, pv_ = pB.next(), pB.next()
                      for h in range(4):
                          tr(pk_[:, 128 * h:128 * h + 128], kf[h][:, cs], ident_f)
                      for h in range(4):
                          tr(pv_[:, 128 * h:128 * h + 128], vf[h][:, cs], ident_f)
                      vb, kbg, kdec = H(), H(), H()
                      tt(v4(vb), v4(pv_), bcol.bc(1, 128), ALU.mult)
                      tt(v4(kbg), v4(pk_), bk.bc(1, 128), ALU.mult)
                      tt(v4(kdec), v4(pk_), kd.bc(1, 128), ALU.mult)
                      pkk, pqk = pB.next(), pB.next()
                      for h in range(4):
                          mm(pkk[:, 128 * h:128 * h + 128], kb16[h][:, cs], kb16[h][:, cs])
                      for h in range(4):
                          mm(pqk[:, 128 * h:128 * h + 128], kb16[h][:, cs], qb[h][:, cs])
                      Dm, E1 = decs[ci]
                      Mf, QKm = QF(), H()
                      tt(rr(Mf[:, 0:512]), pkk[:], E1[:, 0:512], ALU.mult)
                      tt(QKm[:], pqk[:], Dm[:, 0:512], ALU.mult)
                      free(Dm, E1)
                      pL = pB.next()
                      for h in range(4):
                          tr(pL[:, 128 * h:128 * h + 128], f32v(Mf[:, 128 * h:128 * h + 128]), ident_f)
                      P_ = QF()
                      cp(rr(P_[:, 0:512]), pL[:])
                      Xf = QF()
                      tt(v4(Xf), ident_f.bc(0, 4), f32v(v4(Mf)), ALU.subtract)
                      Q_ = Mf
                      for st in range(1, nsteps + 1):
                          next(dec_next, None)
                          pP = pB.next()
                          for h in range(4):
                              hs = slice(128 * h, 128 * h + 128)
                              mm(pP[:, hs], rr(Q_[:, hs]), rr(P_[:, hs]))
                          nP = QF()
                          cp(rr(nP[:, 0:512]), pP[:])
                          nQ = None
                          if st < nsteps:
                              pQ = pB.next()
                              for h in range(4):
                                  hs = slice(128 * h, 128 * h + 128)
                                  mm(pQ[:, hs], rr(P_[:, hs]), rr(Q_[:, hs]))
                              nQ = QF()
                              cp(rr(nQ[:, 0:512]), pQ[:], eng="vector")
                          pX = pB.next()
                          for h in range(4):
                              hs = slice(128 * h, 128 * h + 128)
                              mm(pX[:, hs], rr(nP[:, hs]), rr(Xf[:, hs]))
                          nX = QF()
                          tt(nX[:, 0:512], pX[:], f32v(Xf[:, 0:512]), ALU.add)
                          free(P_, Q_, Xf)
                          P_, Q_, Xf = nP, nQ, nX
                      free(P_)
                      for _ in dec_next:
                          pass
                      X = H()
                      cp(X[:], f32v(Xf[:, 0:512]))
                      free(Xf)
                      pw = pB.next()
                      for h in range(4):
                          hs = slice(128 * h, 128 * h + 128)
                          mm(pw[:, hs], kbg[:, hs], X[:, hs])
                      nwT = H()
                      act(nwT[:], pw[:], AF.Copy, scale=-1.0)
                      free(kbg)
                      if prompt:
                          pvn = pB.next()
                          for h in range(4):
                              hs = slice(128 * h, 128 * h + 128)
                              mm(pvn[:, hs], X[:, hs], vb[:, hs], start=True, stop=False)
                              mm(pvn[:, hs], nwT[:, hs], Sp_b[l][:, hs], start=False, stop=True)
                          free(vb, nwT, X)
                          vn = H()
                          cp(vn[:], pvn[:])
                          po = pB.next()
                          for h in range(4):
                              hs = slice(128 * h, 128 * h + 128)
                              mm(po[:, hs], vn[:, hs], QKm[:, hs], start=True, stop=False)
                              mm(po[:, hs], Sp_b[l][:, hs], qgb[h][:, cs], start=False, stop=True)
                          free(QKm)
                          cp(rv(OR)[:, :, cs], v4(po))
                          pS = pB.next()
                          for h in range(4):
                              hs = slice(128 * h, 128 * h + 128)
                              mm(pS[:, hs], kdec[:, hs], vn[:, hs])
                          tt(v4(Sp[l]), v4(Sp[l]), egl3[:, :, ci].bc(1, 128), ALU.mult)
                          tt(Sp[l][:], Sp[l][:], pS[:], ALU.add)
                          cp(Sp_b[l][:], Sp[l][:])
                          if last_p and ci == nch - 1:
                              dma(dram(o_p_gds[l].rearrange("h k v -> k h v")), v4(Sp[l]), q=IOQ, is_out=True)
                          free(vn)
                      else:
                          for h in range(4):
                              hs = slice(128 * h, 128 * h + 128)
                              SR = R()
                              SR3 = SR[:].re("p (b v) -> p b v", b=NB)
                              dma(SR3, dram(st_gds[l, :, h].rearrange("b k v -> k b v")), q=IOQ)
                              Sb = [H() for _ in range(4)]
                              for bq in range(4):
                                  cp(Sb[bq][:], SR[:, 512 * bq:512 * bq + 512])
                              wz = [H() for _ in range(4)]
                              for bq in range(4):
                                  memset(wz[bq][:], 0.0)
                                  o_ap = Vw(wz[bq], bass.AP(wz[bq].h[:].tensor, wz[bq].h[:].offset + 32 * bq,
                                                            [list(wz[bq].h[:].ap[0]), [136, 4], [1, 8]]))
                                  i_ap = nwT[:, 128 * h + 32 * bq:128 * h + 32 * bq + 32].re("p (b t) -> p b t", b=4)
                                  cp(o_ap, i_ap, eng="vector")
                              pvn = pB.next()
                              mm(pvn[:, 0:128], X[:, hs], vb[:, hs], start=True, stop=False)
                              for b in range(NB):
                                  bs_ = slice(128 * (b % 4), 128 * (b % 4) + 128)
                                  mm(pvn[:, 0:128], wz[b // 4][:, bs_], Sb[b // 4][:, bs_], start=False, stop=(b == NB - 1))
                              free(*wz)
                              vn = H()
                              cp(vn[:, 0:128], pvn[:, 0:128])
                              po = pB.next()
                              mm(po[:, 0:128], vn[:, 0:128], QKm[:, hs], start=True, stop=False)
                              for b in range(NB):
                                  bs_ = slice(128 * (b % 4), 128 * (b % 4) + 128)
                                  mm(po[:, 8 * b:8 * b + 8], Sb[b // 4][:, bs_], qgb[h][:, 8 * b:8 * b + 8], start=False, stop=(b == NB - 1))
                              cp(rv(OR)[:, h, cs], po[:, 0:128])
                              for bq in range(4):
                                  kz4 = H()
                                  tt(v4(kz4), kdec[:, hs].bc(0, 4), bmask[:, 4 * bq:4 * bq + 4].bc(1, 128), ALU.mult)
                                  pS = pB.next()
                                  for bl in range(4):
                                      mm(pS[:, 128 * bl:128 * bl + 128], kz4[:, 128 * bl:128 * bl + 128], vn[:, 0:128])
                                  free(kz4)
                                  srv = SR[:, 512 * bq:512 * bq + 512]
                                  srv4 = srv.re("p (b v) -> p b v", b=4)
                                  tt(srv4, srv4, egl3[:, h, 4 * bq:4 * bq + 4].bc(1, 128), ALU.mult)
                                  tt(srv, srv, pS[:], ALU.add)
                              dma(dram(o_s_gds[l, :, h].rearrange("b k v -> k b v")), SR3, q=IOQ, is_out=True)
                              free(SR, vn, *Sb)
                          free(vb, nwT, X, QKm)
                      free(kdec, cc)
                  free(GC, BR, EGl)
                  free(*kf)
                  free(*vf)
                  free(*qb)
                  free(*qgb)
                  free(*kb16)
                  sqs_, pss_, rsd = [], [], []
                  for h in range(4):
                      sq = H()
                      act(sq[:, 0:N], rv(OR)[:, h, :], AF.Square)
                      sqs_.append(sq)
                  for h in range(4):
                      pn = pB.next()
                      mm(pn[:, 0:N], ones_b[:], sqs_[h][:, 0:N])
                      pss_.append(pn)
                  free(*sqs_)
                  for h in range(4):
                      r_ = F()
                      act(r_[:, 0:N], pss_[h][:, 0:N], AF.Ln, scale=1.0 / 128, bias=eps_c)
                      rsd.append(r_)
                  for h in range(4):
                      act(rsd[h][:, 0:N], rsd[h][:, 0:N], AF.Exp, scale=-0.5)
                  gdn_o = [None] * 4

                  def gd_z(mi, msz, ps):
                      zs = F()
                      act(zs[:, 0:N], ps[:, 0:N], AF.Silu)
                      tt(rsd[mi][:, 0:N], rsd[mi][:, 0:N], rv(OR)[:, mi, :], ALU.mult)
                      gdn_o[mi] = H()
                      stt(gdn_o[mi][:, 0:N], rsd[mi][:, 0:N], pkv(l, "gd_nw"), zs[:, 0:N], ALU.mult, ALU.mult)
                      free(zs, rsd[mi])

                  dense(("w_in", l), KD, C_Z, 512, rhs_h, N, gd_z)
                  free(OR)
                  if l == 0 and gi == 0:
                      for c in range(4):
                          dump("gdn_o%d" % c, gdn_o[c][:, 0:N])

                  ckpt("gdn")
                  merged = [None] * KD
                  for mp in range(0, KD, 2):
                      macc = [F(), F()]
                      for bi, (Wb, yb) in enumerate((("rg_w_out", y_rg), ("s5_w_out", y_s5), ("gdn_w_out", gdn_o))):
                          sg = [None, None]

                          def mg_gate(j_, msz, ps):
                              sg[j_] = F()
                              act(sg[j_][:, 0:N], ps[:, 0:N], AF.Sigmoid)

                          dense(("w_in", l), KD, C_GATE + D * bi + 128 * mp, 256, rhs_h, N, mg_gate)

                          def mg_y(j_, msz, ps):
                              if bi == 0:
                                  tt(macc[j_][:, 0:N], ps[:, 0:N], sg[j_][:, 0:N], ALU.mult)
                              else:
                                  tt(sg[j_][:, 0:N], ps[:, 0:N], sg[j_][:, 0:N], ALU.mult)
                                  if bi == 1:
                                      tt(macc[j_][:, 0:N], macc[j_][:, 0:N], sg[j_][:, 0:N], ALU.add)
                                  else:
                                      merged[mp + j_] = H()
                                      tt(merged[mp + j_][:, 0:N], macc[j_][:, 0:N], sg[j_][:, 0:N], ALU.add)
                              free(sg[j_])

                          dense((Wb, l), 4, 128 * mp, 256, lambda k: yb[k][:, 0:N], N, mg_y)
                      free(*macc)
                  free(*y_rg)
                  free(*y_s5)
                  free(*gdn_o)
                  free(*hT)
                  mixed = [None] * KD

                  def mx(mi, msz, ps):
                      mixed[mi] = F()
                      cp(mixed[mi][:, 0:N], ps[:, 0:N])

                  dense(("w_out", l), KD, 0, D, lambda k: merged[k][:, 0:N], N, mx)
                  free(*merged)
                  rstd = rms_rstd([mixed[k][:, 0:N] for k in range(KD)], N, 1.0 / D)
                  for k in range(KD):
                      stt(mixed[k][:, 0:N], mixed[k][:, 0:N], pkv(l, "g_post", k), rstd[:, 0:N], ALU.mult, ALU.mult)
                      tt(xT[k][:, 0:N], xT[k][:, 0:N], mixed[k][:, 0:N], ALU.add)
                  free(rstd)
                  free(*mixed)
                  if l == 0 and gi == 0:
                      for k in range(KD):
                          dump("xmix%d" % k, xT[k][:, 0:N])

                  ckpt("merge")
                  rstd = rms_rstd([xT[k][:, 0:N] for k in range(KD)], N, 1.0 / D)
                  hf = [H() for _ in range(KD)]
                  for k in range(KD):
                      stt(hf[k][:, 0:N], xT[k][:, 0:N], pkv(l, "g_pref", k), rstd[:, 0:N], ALU.mult, ALU.mult)
                  free(rstd)
                  rhs_f = lambda k: hf[k][:, 0:N]
                  acts = [None] * KF
                  for f0 in range(0, KF, 2):
                      sg = {}

                      def ff_g(mi, msz, ps):
                          sg[mi] = F()
                          act(sg[mi][:, 0:N], ps[:, 0:N], AF.Silu)

                      def ff_u(mi, msz, ps):
                          acts[f0 + mi] = H()
                          tt(acts[f0 + mi][:, 0:N], ps[:, 0:N], sg[mi][:, 0:N], ALU.mult)
                          free(sg[mi])

                      dense(("ffn_w_gate", l), KD, 128 * f0, 256, rhs_f, N, ff_g)
                      dense(("ffn_w_up", l), KD, 128 * f0, 256, rhs_f, N, ff_u)
                  free(*hf)
                  fo = [None] * KD

                  def ff_d(mi, msz, ps):
                      fo[mi] = F()
                      cp(fo[mi][:, 0:N], ps[:, 0:N])

                  dense(("ffn_w_down", l), KF, 0, D, lambda k: acts[k][:, 0:N], N, ff_d)
                  free(*acts)
                  rstd = rms_rstd([fo[k][:, 0:N] for k in range(KD)], N, 1.0 / D)
                  for k in range(KD):
                      stt(fo[k][:, 0:N], fo[k][:, 0:N], pkv(l, "g_postf", k), rstd[:, 0:N], ALU.mult, ALU.mult)
                      tt(xT[k][:, 0:N], xT[k][:, 0:N], fo[k][:, 0:N], ALU.add)
                  free(rstd)
                  free(*fo)
                  if l == 0 and gi == 0:
                      for k in range(KD):
                          dump("xl0_%d" % k, xT[k][:, 0:N])

              yo = yT_out.rearrange("(k p) n -> p k n", p=128)
              for k in range(KD):
                  dma(dram(yo[:, k, tok0:tok0 + N]), xT[k][:, 0:N], q=IOQ, is_out=True)
              free(*xT)

        except _Stop:
            pass
        S.wait_all("sync", S.out_deps)
        S.emit()
        info = {"sbuf_bytes": S.sb_bytes, "counts": {e: S.count[e] for e in S.ENGS}, "pools": S.pool_n}
    return nc, info


def _consts():
    c = np.zeros((128, 128 * 7 + 20), np.float32)
    j = np.arange(128)[:, None]
    i = np.arange(128)[None, :]
    c[:, 0:128] = (j == i)
    c[:, 128:256] = (j <= i)
    c[:, 256:384] = (j < i)
    c[:, 384:512] = 1.0
    same = (j // 8) == (i // 8)
    c[:, 512:640] = same & (j <= i)
    c[:, 640:768] = same & (j < i)
    c[:, 768:896] = same
    c[:, 896:912] = (j // 8) == np.arange(16)[None, :]
    c[:, 912:916] = (j // 32) == np.arange(4)[None, :]
    return c


def _fm(v):
    return np.ascontiguousarray(v.reshape(-1, 128).T)


def _pack_layer(l, p):
    pk = np.zeros((128, NPK), np.float32)

    def put(name, arr):
        o, n = PK[name]
        pk[:, o:o + n] = arr.reshape(128, n)

    put("g_pre", _fm(p["norm_pre_mix"][l]))
    put("g_post", _fm(p["norm_post_mix"][l]))
    put("g_pref", _fm(p["norm_pre_ffn"][l]))
    put("g_postf", _fm(p["norm_post_ffn"][l]))
    put("rg_cw", p["rg_conv_w"][l].reshape(4, 4, 128).transpose(2, 1, 0))
    put("rg_cb", _fm(p["rg_conv_b"][l]))
    put("rg_ba", _fm(p["rg_ba"][l]))
    put("rg_bx", _fm(p["rg_bx"][l]))
    put("rg_lam", _fm(p["rg_lambda"][l]))
    put("s5_d", _fm(p["s5_d"][l]))
    put("s5_glub", _fm(p["s5_glu_b"][l]))
    put("gd_cw", p["gdn_conv_w"][l].reshape(4, 12, 128).transpose(2, 1, 0))
    put("gd_nw", p["gdn_norm_w"][l].reshape(128, 1))
    st = lambda a: a.reshape(16, 2, 64).transpose(1, 2, 0).reshape(128, 16)
    put("s5_are", st(p["s5_a_re"][l]))
    put("s5_aim", st(p["s5_a_im"][l]))
    put("s5_ldt", st(np.repeat(p["s5_log_dt"][l][:, None], 64, axis=1)))
    put("gd_alog", np.repeat(p["gdn_a_log"][l][None, :], 128, axis=0))
    put("gd_dtb", np.repeat(p["gdn_dt_bias"][l][None, :], 128, axis=0))
    put("w_ab", p["w_in"][l][:, C_AB:C_AB + 8].reshape(8, 128, 8).transpose(1, 0, 2))
    rgw = np.zeros((128, 2, 4, 128), np.float32)
    for a, w in enumerate((p["rg_wa"][l], p["rg_wx"][l])):
        for c in range(4):
            for n2 in range(2):
                rgw[n2 * 64:(n2 + 1) * 64, a, c, n2 * 64:(n2 + 1) * 64] = w[2 * c + n2]
    s5b = np.zeros((128, 5, 4, 128), np.float32)
    are, aim, ldt = p["s5_a_re"][l], p["s5_a_im"][l], p["s5_log_dt"][l]
    for sc in range(16):
        uc, q = sc // 4, sc % 4
        for g2 in range(2):
            g = 2 * sc + g2
            r0 = q * 32 + g2 * 16
            s5b[r0:r0 + 16, 0, uc, g2 * 64:(g2 + 1) * 64] = p["s5_b_re"][l][g].T
            s5b[r0:r0 + 16, 1, uc, g2 * 64:(g2 + 1) * 64] = p["s5_b_im"][l][g].T
            s5b[q * 32:q * 32 + 32, 2, uc, g2 * 64:(g2 + 1) * 64] = are[g][None, :]
            s5b[q * 32:q * 32 + 32, 3, uc, g2 * 64:(g2 + 1) * 64] = aim[g][None, :]
            s5b[q * 32:q * 32 + 32, 4, uc, g2 * 64:(g2 + 1) * 64] = ldt[g]
    s5c = np.zeros((128, 2, 16, 128), np.float32)
    for sc in range(16):
        for g2 in range(2):
            g = 2 * sc + g2
            c0 = ((sc % 4) * 2 + g2) * 16
            s5c[g2 * 64:(g2 + 1) * 64, 0, sc, c0:c0 + 16] = p["s5_c_re"][l][g].T
            s5c[g2 * 64:(g2 + 1) * 64, 1, sc, c0:c0 + 16] = p["s5_c_im"][l][g].T
    return pk, rgw.reshape(128, -1), s5b.reshape(128, -1), s5c.reshape(128, -1)


def make_in_maps(inp):
    p = {k: np.asarray(v, dtype=np.float32) for k, v in inp.items()}
    packs = [_pack_layer(l, p) for l in range(NL)]
    shared = {
        "w_in": p["w_in"], "rg_w_out": p["rg_w_out"], "s5_glu_w": p["s5_glu_w"], "s5_w_out": p["s5_w_out"],
        "gdn_w_out": p["gdn_w_out"], "w_out": p["w_out"], "ffn_w_gate": p["ffn_w_gate"], "ffn_w_up": p["ffn_w_up"],
        "ffn_w_down": p["ffn_w_down"],
        "pk": np.stack([x[0] for x in packs]), "rgw": np.stack([x[1] for x in packs]),
        "s5b": np.stack([x[2] for x in packs]), "s5c": np.stack([x[3] for x in packs]),
        "cst": _consts(),
    }
    maps = []
    for c in range(8):
        bs = slice(NB * c, NB * c + NB)
        xT = np.concatenate([p["x_prompt"][c].T, p["x_sample"][bs].reshape(NS_TOK, D).T], axis=1)
        s5 = lambda a: a[:, bs].reshape(NL, NB, 16, 2, 64).transpose(0, 3, 4, 2, 1).reshape(NL, 128, 16, NB)
        m = dict(shared)
        m.update({
            "xT": np.ascontiguousarray(xT),
            "st_rgconv": np.ascontiguousarray(p["state_rg_conv"][:, bs].transpose(0, 3, 1, 2)),
            "st_rgh": np.ascontiguousarray(p["state_rg_h"][:, bs].transpose(0, 2, 1)),
            "st_s5re": np.ascontiguousarray(s5(p["state_s5_re"])),
            "st_s5im": np.ascontiguousarray(s5(p["state_s5_im"])),
            "st_gdconv": np.ascontiguousarray(p["state_gdn_conv"][:, bs].transpose(0, 3, 1, 2)),
            "st_gds": np.ascontiguousarray(p["state_gdn_s"][:, bs]),
        })
        maps.append(m)
    return maps


def assemble(results):
    f = np.float32
    y_prompt = np.zeros((8, NP_TOK, D), f)
    y_sample = np.zeros((128, 8, D), f)
    p_rg_conv = np.zeros((NL, 8, 3, 512), f)
    p_rg_h = np.zeros((NL, 8, 512), f)
    p_s5_re = np.zeros((NL, 8, 32, 64), f)
    p_s5_im = np.zeros((NL, 8, 32, 64), f)
    p_gd_conv = np.zeros((NL, 8, 3, 1536), f)
    p_gd_s = np.zeros((NL, 8, 4, 128, 128), f)
    s_rg_conv = np.zeros((NL, 128, 3, 512), f)
    s_rg_h = np.zeros((NL, 128, 512), f)
    s_s5_re = np.zeros((NL, 128, 32, 64), f)
    s_s5_im = np.zeros((NL, 128, 32, 64), f)
    s_gd_conv = np.zeros((NL, 128, 3, 1536), f)
    s_gd_s = np.zeros((NL, 128, 4, 128, 128), f)
    for c, r in enumerate(results):
        bs = slice(NB * c, NB * c + NB)
        yT = r["yT"]
        y_prompt[c] = yT[:, :NP_TOK].T
        y_sample[bs] = yT[:, NP_TOK:].T.reshape(NB, 8, D)
        p_rg_conv[:, c] = r["o_p_rgconv"].transpose(0, 2, 1)
        p_rg_h[:, c] = r["o_p_rgh"][:, :, 0]
        un5 = lambda a: a.reshape(NL, 2, 64, 16).transpose(0, 3, 1, 2).reshape(NL, 32, 64)
        p_s5_re[:, c] = un5(r["o_p_s5re"])
        p_s5_im[:, c] = un5(r["o_p_s5im"])
        p_gd_conv[:, c] = r["o_p_gdconv"].transpose(0, 2, 1)
        p_gd_s[:, c] = r["o_p_gds"]
        s_rg_conv[:, bs] = r["o_s_rgconv"].transpose(0, 2, 3, 1)
        s_rg_h[:, bs] = r["o_s_rgh"].transpose(0, 2, 1)
        un5s = lambda a: a.reshape(NL, 2, 64, 16, NB).transpose(0, 4, 3, 1, 2).reshape(NL, NB, 32, 64)
        s_s5_re[:, bs] = un5s(r["o_s_s5re"])
        s_s5_im[:, bs] = un5s(r["o_s_s5im"])
        s_gd_conv[:, bs] = r["o_s_gdconv"].transpose(0, 2, 3, 1)
        s_gd_s[:, bs] = r["o_s_gds"]
    return (y_prompt, y_sample, p_rg_conv, p_rg_h, p_s5_re, p_s5_im, p_gd_conv, p_gd_s,
            s_rg_conv, s_rg_h, s_s5_re, s_s5_im, s_gd_conv, s_gd_s)


_CACHE = {}


def kernel(**inputs):
    if "nc" not in _CACHE:
        _CACHE["nc"] = build_program()[0]
    nc = _CACHE["nc"]
    maps = make_in_maps(inputs)
    res = run_bass_kernel_spmd(nc, maps, core_ids=list(range(8)))
    return assemble(res.results)
```
